# Optimizing a Trainium2 kernel written in Bass

```python
import jax, jax.numpy as jnp
from jax import lax
import numpy as np

D_MODEL = 1024
BATCH = 32
SEQ = 2048
DEPTH = 2

N_MIXERS = 2
N_MEM = 256
HG_HEADS = 8
HG_DK = D_MODEL // HG_HEADS
HG_DV = D_MODEL // HG_HEADS
D_MIX = HG_HEADS * HG_DV
HG_CHUNK = 16
CONV_WIDTH = 3
XA_HEADS = 4
XA_DH = 128
D_XA = XA_HEADS * XA_DH
D_IN = 4 * D_MIX + D_XA
D_CAT = D_MIX + D_XA
N_HGRN = (DEPTH + 1) // 2
N_CONV = DEPTH // 2
DN_ALPHA = (2 * DEPTH) ** 0.25
DN_BETA = (8 * DEPTH) ** -0.25
LN_EPS = 1e-5
RMS_EPS = 1e-5

kernel_name = "hgrn2_shortconv_memxattn_deepnorm_hybrid"


def layer_norm(x, g, b):
    xf = x.astype(jnp.float32)
    mu = jnp.mean(xf, axis=-1, keepdims=True)
    var = jnp.mean(jnp.square(xf - mu), axis=-1, keepdims=True)
    y = (xf - mu) * lax.rsqrt(var + LN_EPS) * g.astype(jnp.float32) + b.astype(jnp.float32)
    return y.astype(x.dtype)


def hgrn2_mixer(zq, zf, zi, zg, lb, norm_w):
    f32 = jnp.float32
    bsz, slen, _ = zq.shape
    dt = zq.dtype
    q = jax.nn.silu(zq.astype(f32))
    f = lb + (1.0 - lb) * jax.nn.sigmoid(zf.astype(f32))
    k = 1.0 - f
    logf = jnp.log(f)
    v = zi.astype(f32)
    nc = slen // HG_CHUNK

    def to_chunks(t, d):
        return t.reshape(bsz, nc, HG_CHUNK, HG_HEADS, d).transpose(1, 0, 3, 2, 4)

    qc, kc, vc, gc = to_chunks(q, HG_DK), to_chunks(k, HG_DK), to_chunks(v, HG_DV), to_chunks(logf, HG_DK)
    causal = jnp.tril(jnp.ones((HG_CHUNK, HG_CHUNK), dtype=bool))[:, :, None]

    def step(state, inp):
        qq, kk, vv, gg = inp
        b = jnp.cumsum(gg, axis=2)
        o_inter = jnp.einsum('bhtk,bhkv->bhtv', qq * jnp.exp(b), state)
        diff = b[:, :, :, None, :] - b[:, :, None, :, :]
        decay = jnp.exp(jnp.where(causal, diff, -jnp.inf))
        scores = jnp.einsum('bhtk,bhsk,bhtsk->bhts', qq, kk, decay)
        o_intra = jnp.einsum('bhts,bhsv->bhtv', scores, vv)
        b_last = b[:, :, -1, :]
        k_dec = kk * jnp.exp(b_last[:, :, None, :] - b)
        new_state = jnp.exp(b_last)[..., None] * state + jnp.einsum('bhsk,bhsv->bhkv', k_dec, vv)
        return new_state, o_inter + o_intra

    s0 = jnp.zeros((bsz, HG_HEADS, HG_DK, HG_DV), f32)
    _, o = lax.scan(step, s0, (qc, kc, vc, gc))
    o = o.transpose(1, 0, 3, 2, 4).reshape(bsz, slen, HG_HEADS, HG_DV)
    o = o * lax.rsqrt(jnp.mean(jnp.square(o), axis=-1, keepdims=True) + RMS_EPS) * norm_w.astype(f32)
    o = o.reshape(bsz, slen, D_MIX) * jax.nn.silu(zg.astype(f32))
    return o.astype(dt)


def short_conv_mixer(zb, zc, zh, zg, conv_w):
    u = zc * zh
    w = conv_w[:, None, :].astype(u.dtype)
    c = lax.conv_general_dilated(u, w, window_strides=(1,), padding=[(CONV_WIDTH - 1, 0)],
                                 dimension_numbers=('NWC', 'WIO', 'NWC'),
                                 feature_group_count=D_MIX)
    return zb * c * jax.nn.silu(zg)


def mem_cross_attn(zq, mem_k, mem_v):
    bsz, slen, _ = zq.shape
    q = zq.reshape(bsz, slen, XA_HEADS, XA_DH)
    s = jnp.einsum('bshd,bmhd->bhsm', q, mem_k).astype(jnp.float32) * (XA_DH ** -0.5)
    p = jax.nn.softmax(s, axis=-1).astype(zq.dtype)
    o = jnp.einsum('bhsm,bmhd->bshd', p, mem_v)
    return o.reshape(bsz, slen, D_XA)


def setup_inputs(seed: int = 0) -> dict:
    key = jax.random.key(seed)
    ks = jax.random.split(key, 14)
    f32 = jnp.float32
    x = jax.random.normal(ks[0], (BATCH, SEQ, D_MODEL), f32)
    mem = jax.random.normal(ks[1], (BATCH, N_MEM, D_MODEL), f32)
    w_in = jax.random.normal(ks[2], (DEPTH, D_MODEL, D_IN), f32) * D_MODEL ** -0.5
    w_out = jax.random.normal(ks[3], (DEPTH, D_CAT, D_MODEL), f32) * (D_CAT ** -0.5 * DN_BETA)
    ln_g = 1.0 + 0.02 * jax.random.normal(ks[4], (DEPTH, D_MODEL), f32)
    ln_b = 0.02 * jax.random.normal(ks[5], (DEPTH, D_MODEL), f32)
    hgrn_lb_logits = 0.1 * jax.random.normal(ks[6], (N_HGRN + 1, D_MIX), f32)
    hgrn_norm_w = 1.0 + 0.02 * jax.random.normal(ks[7], (N_HGRN, HG_DV), f32)
    conv_w = jax.random.normal(ks[8], (N_CONV, CONV_WIDTH, D_MIX), f32) * CONV_WIDTH ** -0.5
    mem_ln_g = 1.0 + 0.02 * jax.random.normal(ks[9], (D_MODEL,), f32)
    mem_ln_b = 0.02 * jax.random.normal(ks[10], (D_MODEL,), f32)
    w_mem_kv = jax.random.normal(ks[11], (D_MODEL, 2 * D_XA), f32) * D_MODEL ** -0.5
    return {"x": x, "mem": mem, "w_in": w_in, "w_out": w_out, "ln_g": ln_g, "ln_b": ln_b,
            "hgrn_lb_logits": hgrn_lb_logits, "hgrn_norm_w": hgrn_norm_w, "conv_w": conv_w,
            "mem_ln_g": mem_ln_g, "mem_ln_b": mem_ln_b, "w_mem_kv": w_mem_kv}


def reference(x, mem, w_in, w_out, ln_g, ln_b, hgrn_lb_logits, hgrn_norm_w, conv_w,
              mem_ln_g, mem_ln_b, w_mem_kv):
    bsz = x.shape[0]
    kv = layer_norm(mem, mem_ln_g, mem_ln_b) @ w_mem_kv
    mem_k = kv[..., :D_XA].reshape(bsz, N_MEM, XA_HEADS, XA_DH)
    mem_v = kv[..., D_XA:].reshape(bsz, N_MEM, XA_HEADS, XA_DH)
    lb_all = jnp.cumsum(jax.nn.softmax(hgrn_lb_logits.astype(jnp.float32), axis=0), axis=0)[:N_HGRN]
    for layer in range(DEPTH):
        z = x @ w_in[layer]
        za, zb, zc, zg, zx = jnp.split(z, [D_MIX, 2 * D_MIX, 3 * D_MIX, 4 * D_MIX], axis=-1)
        j = layer // N_MIXERS
        if layer % N_MIXERS == 0:
            mix = hgrn2_mixer(za, zb, zc, zg, lb_all[j], hgrn_norm_w[j])
        else:
            mix = short_conv_mixer(za, zb, zc, zg, conv_w[j])
        xa = mem_cross_attn(zx, mem_k, mem_v)
        y = jnp.concatenate([mix, xa], axis=-1) @ w_out[layer]
        x = layer_norm(DN_ALPHA * x + y, ln_g[layer], ln_b[layer])
    return x
```

```python
import numpy as np
from contextlib import ExitStack
import concourse.bass as bass
import concourse.mybir as mybir
from concourse.bass_utils import run_bass_kernel_spmd

F32 = mybir.dt.float32
BF16 = mybir.dt.bfloat16
AF = mybir.ActivationFunctionType
ALU = mybir.AluOpType

D = 1024
DIN = 4608
DCAT = 1536
NMEM = 256
ST = 512
ALPHA = 4.0 ** 0.25
LN_EPS = 1e-5
RMS_EPS = 1e-5
XA_SCALE = 128.0 ** -0.5
NCORES = 8


class Buf:
    def __init__(self, name, t):
        self.name = name
        self.t = t
        self.w = None
        self.r = {}
        self.pe_pend = False
        self.excl = False
        self.dkey = None
        self.dcnt = 0

    def __getitem__(self, idx):
        return self.t[idx]


class KB:
    COMPUTE = ('pe', 'act', 'dve', 'pool')

    def __init__(self, nc, es):
        self.nc = nc
        self.es = es
        self.eng = {'pe': nc.tensor, 'act': nc.scalar, 'dve': nc.vector, 'pool': nc.gpsimd, 'sp': nc.sync}
        self.sem = {}
        for e in self.COMPUTE:
            self.sem[e] = es.enter_context(nc.semaphore('s_' + e))
        self.cnt = {e: 0 for e in self.COMPUTE}
        self.waited = {e: {} for e in self.eng}
        self.pe_pending = []
        self.last = {}
        self.nbuf = 0
        self.ninst = {e: 0 for e in self.eng}

    def sb(self, name, shape, dt, es=None):
        t = (es or self.es).enter_context(self.nc.sbuf_tensor(name, list(shape), dt))
        return Buf(name, t)

    def ps(self, name, shape, dt, es=None):
        t = (es or self.es).enter_context(self.nc.psum_tensor(name, list(shape), dt))
        b = Buf(name, t)
        b.excl = True
        return b

    def dram(self, name, shape, dt):
        t = self.nc.dram_tensor(name, list(shape), dt, kind="Internal")
        return Buf(name, t)

    def _deps(self, e, reads, writes):
        deps = {}

        def add(kv):
            if kv is None:
                return
            k, v = kv
            if deps.get(k, 0) < v:
                deps[k] = v
        for b in reads:
            if b.pe_pend and e != 'pe':
                raise RuntimeError(f"buf {b.name} has unsignalled PE access (reader {e})")
            add(b.w)
            if b.excl:
                for k, v in b.r.items():
                    if k != e:
                        add((k, v))
        for b in writes:
            if b.pe_pend and e != 'pe':
                raise RuntimeError(f"buf {b.name} has unsignalled PE access (writer {e})")
            add(b.w)
            for k, v in b.r.items():
                add((k, v))
        return deps

    def _wait(self, e, deps):
        w = self.waited[e]
        for k, v in deps.items():
            if k == e and e == 'pe':
                continue
            if w.get(k, 0) < v:
                self.eng[e].wait_ge(self.sem[k], v)
                w[k] = v

    def _record(self, key, val, reads, writes):
        for b in reads:
            if b.r.get(key, 0) < val:
                b.r[key] = val
        for b in writes:
            b.w = (key, val)
            b.r = {}
        if self.last.get(key, 0) < val:
            self.last[key] = val

    def op(self, e, reads, writes, fn, sig=True):
        self._wait(e, self._deps(e, reads, writes))
        inst = fn()
        self.ninst[e] += 1
        if e == 'pe' and not sig:
            for b in reads:
                b.pe_pend = True
                self.pe_pending.append((b, 'r'))
            for b in writes:
                b.pe_pend = True
                self.pe_pending.append((b, 'w'))
            return inst
        self.cnt[e] += 1
        n = self.cnt[e]
        inst.then_inc(self.sem[e], 1)
        if e == 'pe' and self.pe_pending:
            pr = [b for b, k in self.pe_pending if k == 'r']
            pw = [b for b, k in self.pe_pending if k == 'w']
            for b, _ in self.pe_pending:
                b.pe_pend = False
            self.pe_pending = []
            self._record('pe', n, pr, pw)
        self._record(e, n, reads, writes)
        return inst

    def dma(self, q, out_ap, in_ap, reads=(), writes=(), **kw):
        owner = writes[0] if writes else reads[0]
        if owner.dkey is None:
            owner.dkey = {}
            owner.dcnt = {}
        qt = 'sw' if q == 'pool' else 'hw'
        if qt not in owner.dkey:
            owner.dkey[qt] = 'd_' + owner.name + '_' + qt
            owner.dcnt[qt] = 0
            self.sem[owner.dkey[qt]] = self.es.enter_context(self.nc.semaphore(owner.dkey[qt]))
            self.nbuf += 1
        self._wait(q, self._deps(q, list(reads), list(writes)))
        inst = self.eng[q].dma_start(out=out_ap, in_=in_ap, **kw)
        self.ninst[q] += 1
        owner.dcnt[qt] += 16
        inst.then_inc(self.sem[owner.dkey[qt]], 16)
        self._record(owner.dkey[qt], owner.dcnt[qt], list(reads), list(writes))
        return inst

    def barrier(self, engines=None):
        for e in (engines or list(self.eng)):
            self._wait(e, dict(self.last))


def build(NSEQ, S, debug=False):
    NT = NSEQ * S
    NSUB = S // ST
    nc = bass.Bass("TRN2", target_bir_lowering=False)
    x_d = nc.dram_tensor("x", [NT, D], F32, kind="ExternalInput").ap()
    mem_d = nc.dram_tensor("mem", [NSEQ * NMEM, D], F32, kind="ExternalInput").ap()
    win_d = nc.dram_tensor("w_in", [2, 9, 128, 4096], F32, kind="ExternalInput").ap()
    wout_d = nc.dram_tensor("w_out", [2, 128, 12 * D], F32, kind="ExternalInput").ap()
    lng_d = nc.dram_tensor("ln_g", [2, D], F32, kind="ExternalInput").ap()
    lnb_d = nc.dram_tensor("ln_b", [2, D], F32, kind="ExternalInput").ap()
    lbl_d = nc.dram_tensor("hgrn_lb_logits", [2, D], F32, kind="ExternalInput").ap()
    nw_d = nc.dram_tensor("hgrn_norm_w", [1, 128], F32, kind="ExternalInput").ap()
    cw_d = nc.dram_tensor("conv_w", [1, 3, D], F32, kind="ExternalInput").ap()
    mg_d = nc.dram_tensor("mem_ln_g", [1, D], F32, kind="ExternalInput").ap()
    mb_d = nc.dram_tensor("mem_ln_b", [1, D], F32, kind="ExternalInput").ap()
    wkv_d = nc.dram_tensor("w_mem_kv", [128, 8 * D], F32, kind="ExternalInput").ap()
    y_d = nc.dram_tensor("y", [NT, D], F32, kind="ExternalOutput").ap()
    dbg = {}
    if debug:
        dbg['kt'] = nc.dram_tensor("dbg_kt", [128, NSEQ * 4 * 256], F32, kind="ExternalOutput").ap()
        dbg['v'] = nc.dram_tensor("dbg_v", [128, NSEQ * 2 * 512], F32, kind="ExternalOutput").ap()
        dbg['win'] = nc.dram_tensor("dbg_win", [128, 4096], F32, kind="ExternalOutput").ap()

    with ExitStack() as es:
        kb = KB(nc, es)
        V, A, P, T = kb.eng['dve'], kb.eng['act'], kb.eng['pool'], kb.eng['pe']

        winB = kb.dram("winB", [2, 9, 128, 4096], BF16)
        woutB = kb.dram("woutB", [2, 128, 12 * D], BF16)

        ident = kb.sb("ident", [128, 128], BF16)
        ones = kb.sb("ones", [128, 128], BF16)
        cmask = kb.sb("cmask", [128, 128], F32)
        lng = [kb.sb(f"lng{l}", [128, D], F32) for l in range(2)]
        lnb = [kb.sb(f"lnb{l}", [128, D], F32) for l in range(2)]
        KT = kb.sb("KT", [128, NSEQ, 4, 256], BF16)
        VV = kb.sb("VV", [128, NSEQ, 2, 512], BF16)
        lbp = kb.sb("lbp", [128, 3, 8], F32)
        convw = kb.sb("convw", [128, 3, 8], F32)
        normw = kb.sb("normw", [128, 2], F32)
        mhalf = kb.sb("mhalf", [128, 1], F32)
        epsln = kb.sb("epsln", [128, 1], F32)

        with ExitStack() as pes:
            tmpf = kb.sb("tmpf", [128, 128], F32, pes)
            kb.op('pool', [], [tmpf], lambda: P.memset(tmpf[:], 1.0))
            kb.op('pool', [tmpf], [cmask], lambda: P.affine_select(
                out=cmask[:], in_=tmpf[:], pattern=[[1, 128]], compare_op=ALU.is_ge, fill=0.0,
                base=0, channel_multiplier=-1))
            identf = kb.sb("identf", [128, 128], F32, pes)
            kb.op('pool', [tmpf], [identf], lambda: P.affine_select(
                out=identf[:], in_=tmpf[:], pattern=[[-1, 128]], compare_op=ALU.is_equal, fill=0.0,
                base=0, channel_multiplier=1))
            kb.op('pool', [identf], [ident], lambda: P.tensor_copy(ident[:], identf[:]))
            kb.op('pool', [tmpf], [ones], lambda: P.tensor_copy(ones[:], tmpf[:]))
            kb.op('pool', [], [mhalf], lambda: P.memset(mhalf[:], -0.5))
            kb.op('pool', [], [epsln], lambda: P.memset(epsln[:], LN_EPS))

            for l in range(2):
                kb.dma('sp', lng[l][:], lng_d[l:l + 1, :].partition_broadcast(128), writes=[lng[l]])
                kb.dma('sp', lnb[l][:], lnb_d[l:l + 1, :].partition_broadcast(128), writes=[lnb[l]])
            mlg = kb.sb("mlg", [128, D], F32, pes)
            mlb = kb.sb("mlb", [128, D], F32, pes)
            kb.dma('sp', mlg[:], mg_d[0:1, :].partition_broadcast(128), writes=[mlg])
            kb.dma('sp', mlb[:], mb_d[0:1, :].partition_broadcast(128), writes=[mlb])
            lbl = kb.sb("lbl", [128, 2, 8], F32, pes)
            with nc.allow_non_contiguous_dma(reason="tiny param gathers"):
                kb.dma('sp', lbl[:], lbl_d.rearrange("r (h p) -> p r h", p=128), writes=[lbl])
                kb.dma('sp', convw[:], cw_d[0].rearrange("w (c p) -> p w c", p=128), writes=[convw])
                kb.dma('sp', normw[:, 0:1], nw_d.rearrange("o p -> p o"), writes=[normw])
            kb.op('dve', [lbl], [lbl], lambda: V.tensor_tensor(out=lbl[:, 1, :], in0=lbl[:, 0, :], in1=lbl[:, 1, :], op=ALU.subtract))
            kb.op('act', [lbl], [lbp], lambda: A.activation(out=lbp[:, 0, :], in_=lbl[:, 1, :], func=AF.Sigmoid))
            kb.op('dve', [lbp], [lbp], lambda: V.tensor_scalar(out=lbp[:, 0, :], in0=lbp[:, 0, :], scalar1=-0.5, scalar2=0.5, op0=ALU.mult, op1=ALU.add))
            kb.op('dve', [lbp], [lbp], lambda: V.tensor_scalar(out=lbp[:, 1, :], in0=lbp[:, 0, :], scalar1=-1.0, scalar2=1.0, op0=ALU.mult, op1=ALU.add))
            kb.op('dve', [lbp], [lbp], lambda: V.tensor_scalar(out=lbp[:, 2, :], in0=lbp[:, 0, :], scalar1=-1.0, scalar2=None, op0=ALU.mult))
            kb.op('dve', [normw], [normw], lambda: V.tensor_scalar(out=normw[:, 1:2], in0=normw[:, 0:1], scalar1=1.0, scalar2=None, op0=ALU.mult))

            wkv = kb.sb("wkv", [128, 8, D], BF16, pes)
            kb.dma('pool', wkv[:].rearrange("p k f -> p (k f)"), wkv_d[:, :], writes=[wkv])

            mt = [kb.sb(f"mt{i}", [128, D], F32, pes) for i in range(2)]
            mtb = [kb.sb(f"mtb{i}", [128, D], BF16, pes) for i in range(2)]
            mstat = kb.sb("mstat", [128, 2, 6], F32, pes)
            mmv = kb.sb("mmv", [128, 4], F32, pes)
            memT = kb.sb("memT", [128, 8, 256], BF16, pes)
            pt_tr = kb.ps("pp_tr", [128, 8, 128], BF16, pes)
            pt_a = kb.ps("pp_a", [128, 512], F32, pes)
            for sq in range(NSEQ):
                for mc in range(2):
                    i = (sq * 2 + mc) % 2
                    m, mb_ = mt[i], mtb[i]
                    kb.dma('sp', m[:], mem_d[sq * 256 + mc * 128: sq * 256 + (mc + 1) * 128, :], writes=[m])
                    for hf in range(2):
                        kb.op('dve', [m], [mstat], lambda hf=hf: V.bn_stats(out=mstat[:, hf, :], in_=m[:, hf * 512:(hf + 1) * 512]))
                    kb.op('dve', [mstat], [mmv], lambda: V.bn_aggr(out=mmv[:, 0:2], in_=mstat[:].rearrange("p a b -> p (a b)")))
                    kb.op('dve', [mmv], [mmv], lambda: V.tensor_scalar(out=mmv[:, 1:2], in0=mmv[:, 1:2], scalar1=LN_EPS, scalar2=None, op0=ALU.add))
                    kb.op('pool', [mmv, mhalf], [mmv], lambda: P.tensor_tensor(out=mmv[:, 2:3], in0=mmv[:, 1:2], in1=mhalf[:], op=ALU.pow))
                    kb.op('dve', [mmv], [mmv], lambda: V.scalar_tensor_tensor(out=mmv[:, 3:4], in0=mmv[:, 0:1], scalar=-1.0, in1=mmv[:, 2:3], op0=ALU.mult, op1=ALU.mult))
                    kb.op('act', [m, mmv], [m], lambda: A.activation(out=m[:], in_=m[:], func=AF.Identity, scale=mmv[:, 2:3], bias=mmv[:, 3:4]))
                    kb.op('dve', [m, mlg], [m], lambda: V.tensor_tensor(out=m[:], in0=m[:], in1=mlg[:], op=ALU.mult))
                    kb.op('dve', [m, mlb], [mb_], lambda: V.tensor_tensor(out=mb_[:], in0=m[:], in1=mlb[:], op=ALU.add))
                    for kc in range(8):
                        kb.op('pe', [mb_, ident], [pt_tr], lambda kc=kc: T.transpose(out=pt_tr[:, kc, :], in_=mb_[:, kc * 128:(kc + 1) * 128], identity=ident[:]), sig=(kc == 7))
                    kb.op('act', [pt_tr], [memT], lambda: A.copy(out=memT[:, :, mc * 128:(mc + 1) * 128], in_=pt_tr[:]))
                for h in range(4):
                    for kc in range(8):
                        kb.op('pe', [wkv, memT], [pt_a], lambda h=h, kc=kc: T.matmul(pt_a[:, 0:256], lhsT=wkv[:, kc, h * 128:(h + 1) * 128], rhs=memT[:, kc, :], start=(kc == 0), stop=(kc == 7)), sig=(kc == 7))
                    kb.op('dve', [pt_a], [KT], lambda h=h: V.tensor_copy(KT[:, sq, h, :], pt_a[:, 0:256]))
                for mc in range(2):
                    for kc in range(8):
                        kb.op('pe', [wkv, memT], [pt_a], lambda mc=mc, kc=kc: T.matmul(pt_a[:], lhsT=memT[:, kc, mc * 128:(mc + 1) * 128], rhs=wkv[:, kc, 512:1024], start=(kc == 0), stop=(kc == 7)), sig=(kc == 7))
                    kb.op('act', [pt_a], [VV], lambda mc=mc: A.copy(out=VV[:, sq, mc, :], in_=pt_a[:]))

            kb.barrier()

        rmseps = kb.sb("rmseps", [128, 1], F32)
        kb.op('pool', [], [rmseps], lambda: P.memset(rmseps[:], RMS_EPS))
        xres = [kb.sb(f"xres{i}", [128, D], F32) for i in range(4)]
        xin = kb.sb("xin", [128, D], F32)
        xbb = [kb.sb(f"xbb{i}", [128, D], BF16) for i in range(2)]
        xsb = [kb.sb(f"xsb{i}", [128, D], BF16) for i in range(4)]
        rbuf = [kb.sb(f"rbuf{i}", [128, D], F32) for i in range(2)]
        xTa = [kb.sb(f"xTa{i}", [128, 8, ST], BF16) for i in range(2)]
        xTb = kb.sb("xTb", [128, 8, ST], BF16)
        catT = kb.sb("catT", [128, 12, ST], BF16)
        wring = [kb.sb(f"wring{i}", [128, 8, 4, 128], BF16) for i in range(3)]
        wo = kb.sb("wo", [128, 12, D], BF16)
        FT = [kb.sb(f"ft{i}", [128, ST + 4], F32) for i in range(12)]
        BT = [kb.sb(f"bt{i}", [128, ST], BF16) for i in range(14)]
        Wst = [[kb.sb(f"W{h}_{i}", [128, 128], F32) for i in range(2)] for h in range(8)]
        bcar = kb.sb("bcar", [128, 8], F32)
        rho = kb.sb("rho", [128, 8, 5], F32)
        dgm = [kb.sb(f"dgm{i}", [128, 8], F32) for i in range(2)]
        halo = kb.sb("halo", [128, 8, 2], F32)
        lnst = [kb.sb(f"lnst{i}", [128, 2, 6], F32) for i in range(2)]
        lnmv = [kb.sb(f"lnmv{i}", [128, 6], F32) for i in range(2)]
        pbank = [kb.ps(f"pb{i}", [128, ST], F32) for i in range(8)]
        pA, pB, pC, pD, pE, pF, pG, pH = pbank
        cnt = {'ln': 0, 'u': 0}
        onesb = ones[:, 0:1].to_broadcast([128, ST])

        slot_seq = [(l, c) for _sq in range(NSEQ) for _st in range(NSUB) for l in range(2) for c in range(9)]
        slot_issued = [0]

        def issue_slots(upto):
            while slot_issued[0] < min(upto, len(slot_seq)):
                i = slot_issued[0]
                l, c = slot_seq[i]
                wb = wring[i % 3]
                if i < 18:
                    kb.dma('pool', wb[:].rearrange("p k a j -> p (k a j)"), win_d[l, c, :, :], writes=[wb])
                    kb.dma('sp', winB.t.ap()[l, c, :, :], wb[:].rearrange("p k a j -> p (k a j)"), reads=[wb], writes=[winB])
                else:
                    kb.dma('sp', wb[:].rearrange("p k a j -> p (k a j)"), winB.t.ap()[l, c, :, :], reads=[winB], writes=[wb])
                slot_issued[0] += 1

        def mm(out_ap, lhsT, rhs, start, stop, reads, writes, sig, tag=None):
            if tag == 'sc':
                kb.op('pe', reads, writes, lambda: T.matmul(out_ap, lhsT=lhsT, rhs=rhs, start=start, stop=stop), sig=sig)
            elif tag == 'ds':
                kb.op('pe', reads, writes, lambda: T.matmul(out_ap, lhsT=lhsT, rhs=rhs, start=start, stop=stop), sig=sig)
            elif tag == 'o1':
                kb.op('pe', reads, writes, lambda: T.matmul(out_ap, lhsT=lhsT, rhs=rhs, start=start, stop=stop), sig=sig)
            elif tag == 'o2':
                kb.op('pe', reads, writes, lambda: T.matmul(out_ap, lhsT=lhsT, rhs=rhs, start=start, stop=stop), sig=sig)
            elif tag == 'v':
                kb.op('pe', reads, writes, lambda: T.matmul(out_ap, lhsT=lhsT, rhs=rhs, start=start, stop=stop), sig=sig)
            elif tag == 'ss':
                kb.op('pe', reads, writes, lambda: T.matmul(out_ap, lhsT=lhsT, rhs=rhs, start=start, stop=stop), sig=sig)
            elif tag == 'xa':
                kb.op('pe', reads, writes, lambda: T.matmul(out_ap, lhsT=lhsT, rhs=rhs, start=start, stop=stop), sig=sig)
            elif tag == 'op':
                kb.op('pe', reads, writes, lambda: T.matmul(out_ap, lhsT=lhsT, rhs=rhs, start=start, stop=stop), sig=sig)
            else:
                kb.op('pe', reads, writes, lambda: T.matmul(out_ap, lhsT=lhsT, rhs=rhs, start=start, stop=stop), sig=sig)

        def proj_fm(pb, wb, part, xT):
            for kc in range(8):
                mm(pb[:], wb[:, kc, part, :], xT[:, kc, :], kc == 0, kc == 7, [wb, xT], [pb], kc == 7)

        def to_featmajor(src, xb, xT, tt, cast_eng):
            if cast_eng == 'act':
                kb.op('act', [src], [xb], lambda: A.copy(out=xb[:], in_=src[:]))
            else:
                kb.op('pool', [src], [xb], lambda: P.tensor_copy(xb[:], src[:]))
            pEb = pE[:].bitcast(BF16).rearrange("p (k t) -> p k t", k=8)
            for kc in range(8):
                kb.op('pe', [xb, ident], [pE], lambda kc=kc: T.transpose(out=pEb[:, kc, :], in_=xb[:, kc * 128:(kc + 1) * 128], identity=ident[:]), sig=(kc == 7))
            if cast_eng == 'act':
                kb.op('act', [pE], [xT], lambda: A.copy(out=xT[:, :, tt * 128:(tt + 1) * 128], in_=pEb))
            else:
                kb.op('dve', [pE], [xT], lambda: V.tensor_copy(xT[:, :, tt * 128:(tt + 1) * 128], pEb))

        def ln_stats(rb, mv, st_):
            for half in range(2):
                kb.op('dve', [rb], [st_], lambda: V.bn_stats(out=st_[:, half, :], in_=rb[:, half * 512:(half + 1) * 512]))
            kb.op('dve', [st_], [mv], lambda: V.bn_aggr(out=mv[:, 0:2], in_=st_[:].rearrange("p a b -> p (a b)")))
            kb.op('dve', [mv], [mv], lambda: V.tensor_scalar(out=mv[:, 1:2], in0=mv[:, 1:2], scalar1=LN_EPS, scalar2=None, op0=ALU.add))
            kb.op('pool', [mv, mhalf], [mv], lambda: P.tensor_tensor(out=mv[:, 2:3], in0=mv[:, 1:2], in1=mhalf[:], op=ALU.pow))

        def stage_x_load(sq, st, tt):
            tok0 = sq * S + st * ST
            kb.dma('pool', xsb[tt][:], x_d[tok0 + tt * 128: tok0 + (tt + 1) * 128, :], writes=[xsb[tt]])

        def stage_x_tr(sq, st, tt):
            xT = xTa[(sq * NSUB + st) % 2]
            xb = xsb[tt]
            pEb = pE[:].bitcast(BF16).rearrange("p (k t) -> p k t", k=8)
            for kc in range(8):
                kb.op('pe', [xb, ident], [pE], lambda kc=kc: T.transpose(out=pEb[:, kc, :], in_=xb[:, kc * 128:(kc + 1) * 128], identity=ident[:]), sig=(kc == 7))
            kb.op('dve', [pE], [xT], lambda: V.tensor_copy(xT[:, :, tt * 128:(tt + 1) * 128], pEb))

        def stage_x(sq, st):
            for tt in range(4):
                stage_x_load(sq, st, tt)
                stage_x_tr(sq, st, tt)

        def unit_prologue(g):
            issue_slots(g['gslot'] + 3)
            if g['c'] < 4:
                l, q4 = g['l'], g['c']
                wo_v = wo[:, q4 * 3:(q4 + 1) * 3, :].rearrange("p k f -> p (k f)")
                if g['gslot'] < 18:
                    kb.dma('pool', wo_v, wout_d[l, :, q4 * 3 * D:(q4 + 1) * 3 * D], writes=[wo])
                    kb.dma('sp', woutB.t.ap()[l, :, q4 * 3 * D:(q4 + 1) * 3 * D], wo_v, reads=[wo], writes=[woutB])
                else:
                    kb.dma('sp', wo_v, woutB.t.ap()[l, :, q4 * 3 * D:(q4 + 1) * 3 * D], reads=[woutB], writes=[wo])
            if g['l'] == 1 and g['next_sub'] is not None and g['c'] < 8:
                if g['c'] < 4:
                    stage_x_load(*g['next_sub'], g['c'])
                else:
                    stage_x_tr(*g['next_sub'], g['c'] - 4)
            if g['l'] == 0 and g['c'] == 0 and g['st'] == 0:
                for h in range(8):
                    kb.op('pool', [], [Wst[h][0]], lambda: P.memset(Wst[h][0][:], 0.0))
                kb.op('pool', [], [bcar], lambda: P.memset(bcar[:], 0.0))
                kb.op('pool', [], [rho], lambda: P.memset(rho[:], 0.0))
                kb.op('pool', [], [halo], lambda: P.memset(halo[:], 0.0))

        tail = {}

        def flush_tail(which=None):
            order = ['o', 'ss', 'rms']
            upto = 2 if which is None else order.index(which)
            for k in order[:upto + 1]:
                if k in tail:
                    tail.pop(k)()

        def hgrn_unit(g):
            h, xT = g['c'], g['xT']
            wb = wring[g['gslot'] % 3]
            cnt['u'] += 1
            par = cnt['u'] % 2
            qf, thf, lf, Bt, Dt, En = FT[0:6]
            gf = (FT[6], FT[7], FT[11])[cnt['u'] % 3]
            rs, t1 = FT[8], FT[9]
            vb, qtb, ktb = BT[0 + par], BT[2 + par], BT[4 + par]
            ktok, Ab, Ub, sqb = BT[6], BT[7], BT[8], BT[9]
            dg = dgm[par]
            unit_prologue(g)
            proj_fm(pB, wb, 1, xT)
            kb.op('act', [pB], [thf], lambda: A.activation(out=thf[:, 0:ST], in_=pB[:], func=AF.Tanh, scale=0.5))
            flush_tail('o')
            yield 'F'
            proj_fm(pC, wb, 3, xT)
            kb.op('act', [pC], [gf], lambda: A.activation(out=gf[:, 0:ST], in_=pC[:], func=AF.Silu))
            yield 'F'
            proj_fm(pA, wb, 0, xT)
            kb.op('act', [pA], [qf], lambda: A.activation(out=qf[:, 0:ST], in_=pA[:], func=AF.Silu))
            yield 'F'
            kb.op('act', [thf, lbp], [lf], lambda: A.activation(out=lf[:, 0:ST], in_=thf[:, 0:ST], func=AF.Ln, scale=lbp[:, 0, h:h + 1], bias=lbp[:, 1, h:h + 1]))
            kb.op('dve', [lf, ones, bcar], [Bt], lambda: V.tensor_tensor_scan(out=Bt[:, 0:ST], data0=onesb, data1=lf[:, 0:ST], initial=bcar[:, h:h + 1], op0=ALU.mult, op1=ALU.add))
            B3 = Bt[:, 0:ST].rearrange("p (c t) -> p c t", c=4)
            kb.op('dve', [Bt], [rho], lambda: V.tensor_copy(rho[:, h, 1:5], B3[:, :, 64]))
            kb.op('dve', [Bt, rho], [Dt], lambda: V.tensor_tensor(out=Dt[:, 0:ST].rearrange("p (c t) -> p c t", c=4), in0=B3,
                                                                   in1=rho[:, h, 1:5].unsqueeze(2).to_broadcast([128, 4, 128]), op=ALU.subtract))
            flush_tail('rms')
            kb.op('act', [Dt], [En], lambda: A.activation(out=En[:, 0:ST], in_=Dt[:, 0:ST], func=AF.Exp, scale=-1.0))
            kb.op('act', [Dt], [Dt], lambda: A.activation(out=Dt[:, 0:ST], in_=Dt[:, 0:ST], func=AF.Exp))
            kb.op('dve', [Bt], [bcar], lambda: V.tensor_copy(bcar[:, h:h + 1], Bt[:, ST - 1:ST]))
            kb.op('dve', [rho], [dg], lambda: V.tensor_tensor(out=dg[:, 0:4], in0=rho[:, h, 1:5], in1=rho[:, h, 0:4], op=ALU.subtract))
            kb.op('dve', [rho], [rho], lambda: V.tensor_copy(rho[:, h, 0:1], rho[:, h, 4:5]))
            kb.op('act', [dg], [dg], lambda: A.activation(out=dg[:, 4:8], in_=dg[:, 0:4], func=AF.Exp))
            kb.op('dve', [thf, En], [ktb], lambda: V.scalar_tensor_tensor(out=ktb[:], in0=thf[:, 0:ST], scalar=-1.0, in1=En[:, 0:ST], op0=ALU.add, op1=ALU.mult))
            kb.op('dve', [qf, Dt, lbp], [qtb], lambda: V.scalar_tensor_tensor(out=qtb[:], in0=qf[:, 0:ST], scalar=lbp[:, 2, h:h + 1], in1=Dt[:, 0:ST], op0=ALU.mult, op1=ALU.mult))
            yield 'F'
            for c in range(4):
                for kc in range(8):
                    mm(pD[:, c * 128:(c + 1) * 128], xT[:, kc, c * 128:(c + 1) * 128], wb[:, kc, 2, :], kc == 0, kc == 7,
                       [wb, xT], [pD], kc == 7 and c == 3, 'v')
            kb.op('act', [pD], [vb], lambda: A.copy(out=vb[:], in_=pD[:]))
            yield 'B'
            yield 'B'
            yield 'B'
            pEb = pE[:].bitcast(BF16)
            for c in range(4):
                kb.op('pe', [ktb, ident], [pE], lambda c=c: T.transpose(out=pEb[:, c * 128:(c + 1) * 128], in_=ktb[:, c * 128:(c + 1) * 128], identity=ident[:]), sig=(c == 3))
            kb.op('dve', [pE], [ktok], lambda: V.tensor_copy(ktok[:], pEb[:, 0:ST]))
            for c in range(4):
                cs = slice(c * 128, (c + 1) * 128)
                mm(pF[:, cs], ktb[:, cs], qtb[:, cs], True, True, [ktb, qtb], [pF], c == 3, 'sc')
            kb.op('dve', [pF, cmask], [Ab], lambda: V.tensor_tensor(out=Ab[:].rearrange("p (c t) -> p c t", c=4), in0=pF[:].rearrange("p (c t) -> p c t", c=4),
                                                                    in1=cmask[:].unsqueeze(1).to_broadcast([128, 4, 128]), op=ALU.mult))
            for c in range(4):
                cs = slice(c * 128, (c + 1) * 128)
                mm(pG[:, cs], ktok[:, cs], vb[:, cs], True, True, [ktok, vb], [pG], True, 'ds')
            for c in range(4):
                cs = slice(c * 128, (c + 1) * 128)
                wp, wn = Wst[h][c % 2], Wst[h][(c + 1) % 2]
                kb.op('dve', [wp, dg], [Ub], lambda: V.tensor_scalar(out=Ub[:, cs], in0=wp[:], scalar1=dg[:, 4 + c:5 + c], scalar2=None, op0=ALU.mult))
                kb.op('dve', [wp, dg, pG], [wn], lambda: V.scalar_tensor_tensor(out=wn[:], in0=wp[:], scalar=dg[:, 4 + c:5 + c], in1=pG[:, cs], op0=ALU.mult, op1=ALU.add))
            yield 'B'
            def tail_o():
                for c in range(4):
                    cs = slice(c * 128, (c + 1) * 128)
                    mm(pH[:, cs], vb[:, cs], Ab[:, cs], True, False, [vb, Ab], [pH], False, 'o1')
                    mm(pH[:, cs], Ub[:, cs], qtb[:, cs], False, True, [Ub, qtb], [pH], c == 3, 'o2')
                kb.op('act', [pH], [sqb], lambda: A.activation(out=sqb[:], in_=pH[:], func=AF.Square))
                kb.op('act', [pH, normw], [t1], lambda: A.activation(out=t1[:, 0:ST], in_=pH[:], func=AF.Copy, scale=normw[:, 1:2]))
                tail['ss'] = tail_ss
                tail['rms'] = tail_rms

            def tail_ss():
                mm(pH[:], ones[:], sqb[:], True, True, [ones, sqb], [pH], True, 'ss')
                kb.op('act', [pH], [rs], lambda: A.copy(out=rs[:, 0:ST], in_=pH[:]))

            def tail_rms():
                kb.op('act', [rs, rmseps], [rs], lambda: A.activation(out=rs[:, 0:ST], in_=rs[:, 0:ST], func=AF.Ln, scale=1.0 / 128.0, bias=rmseps[:, 0:1]))
                kb.op('act', [rs], [rs], lambda: A.activation(out=rs[:, 0:ST], in_=rs[:, 0:ST], func=AF.Exp, scale=-0.5))
                kb.op('dve', [t1, rs], [t1], lambda: V.tensor_tensor(out=t1[:, 0:ST], in0=t1[:, 0:ST], in1=rs[:, 0:ST], op=ALU.mult))
                kb.op('pool', [t1, gf], [catT], lambda: P.tensor_tensor(out=catT[:, h, :], in0=t1[:, 0:ST], in1=gf[:, 0:ST], op=ALU.mult))
            assert not tail
            tail['o'] = tail_o

        def conv_unit(g):
            c, xT = g['c'], g['xT']
            wb = wring[g['gslot'] % 3]
            cnt['u'] += 1
            par = cnt['u'] % 2
            zaf, bcp, ub, sg = FT[0 + par], FT[2 + par], FT[4 + par], FT[6 + par]
            cv, m = FT[8], FT[9]
            flush_tail()
            unit_prologue(g)
            proj_fm(pA, wb, 0, xT)
            kb.op('act', [pA], [zaf], lambda: A.copy(out=zaf[:, 0:ST], in_=pA[:]))
            yield 'F'
            proj_fm(pB, wb, 1, xT)
            kb.op('act', [pB], [bcp], lambda: A.copy(out=bcp[:, 0:ST], in_=pB[:]))
            yield 'F'
            proj_fm(pC, wb, 2, xT)
            kb.op('pool', [halo], [ub], lambda: P.tensor_copy(ub[:, 0:2], halo[:, c, :]))
            kb.op('dve', [pC, bcp], [ub], lambda: V.tensor_tensor(out=ub[:, 2:ST + 2], in0=pC[:], in1=bcp[:, 0:ST], op=ALU.mult))
            kb.op('pool', [ub], [halo], lambda: P.tensor_copy(halo[:, c, :], ub[:, ST:ST + 2]))
            yield 'F'
            proj_fm(pD, wb, 3, xT)
            kb.op('act', [pD], [sg], lambda: A.activation(out=sg[:, 0:ST], in_=pD[:], func=AF.Silu))
            yield 'B'
            kb.op('act', [ub, convw], [cv], lambda: A.activation(out=cv[:, 0:ST], in_=ub[:, 2:ST + 2], func=AF.Copy, scale=convw[:, 2, c:c + 1]))
            kb.op('dve', [ub, convw, cv], [cv], lambda: V.scalar_tensor_tensor(out=cv[:, 0:ST], in0=ub[:, 1:ST + 1], scalar=convw[:, 1, c:c + 1], in1=cv[:, 0:ST], op0=ALU.mult, op1=ALU.add))
            yield 'B'
            kb.op('dve', [ub, convw, cv], [cv], lambda: V.scalar_tensor_tensor(out=cv[:, 0:ST], in0=ub[:, 0:ST], scalar=convw[:, 0, c:c + 1], in1=cv[:, 0:ST], op0=ALU.mult, op1=ALU.add))
            kb.op('pool', [zaf, sg], [m], lambda: P.tensor_tensor(out=m[:, 0:ST], in0=zaf[:, 0:ST], in1=sg[:, 0:ST], op=ALU.mult))
            yield 'B'
            kb.op('dve', [m, cv], [catT], lambda: V.tensor_tensor(out=catT[:, c, :], in0=m[:, 0:ST], in1=cv[:, 0:ST], op=ALU.mult))

        def xattn_unit(g):
            h, xT, sq = g['h'], g['xT'], g['sq']
            wb = wring[g['gslot'] % 3]
            cnt['u'] += 1
            par = cnt['u'] % 2
            qx, E0, E1, rden = BT[10 + par], BT[12], BT[13], FT[10]
            flush_tail()
            if h == 0:
                unit_prologue(g)
            proj_fm(pA, wb, h, xT)
            kb.op('act', [pA], [qx], lambda: A.copy(out=qx[:], in_=pA[:]))
            yield 'F'
            for mc, (pb, Eb) in enumerate(((pB, E0), (pC, E1))):
                mm(pb[:], KT[:, sq, h, mc * 128:(mc + 1) * 128], qx[:], True, True, [KT, qx], [pb], True)
                kb.op('act', [pb], [Eb], lambda: A.activation(out=Eb[:], in_=pb[:], func=AF.Exp, scale=XA_SCALE))
            yield 'B'
            for mc, Eb in enumerate((E0, E1)):
                mm(pD[:], VV[:, sq, mc, h * 128:(h + 1) * 128], Eb[:], mc == 0, mc == 1, [VV, Eb], [pD], mc == 1)
            for mc, Eb in enumerate((E0, E1)):
                mm(pF[:], ones[:], Eb[:], mc == 0, mc == 1, [ones, Eb], [pF], mc == 1)
            kb.op('act', [pF], [rden], lambda: A.activation(out=rden[:, 0:ST], in_=pF[:], func=AF.Ln))
            kb.op('act', [rden], [rden], lambda: A.activation(out=rden[:, 0:ST], in_=rden[:, 0:ST], func=AF.Exp, scale=-1.0))
            kb.op('dve', [pD, rden], [catT], lambda: V.tensor_tensor(out=catT[:, 8 + h, :], in0=pD[:], in1=rden[:, 0:ST], op=ALU.mult))

        def outproj_unit(g):
            l, tt, tok0 = g['l'], g['tt'], g['tok0']
            flush_tail()
            if l == 0 and tt == 0:
                for t4 in range(4):
                    kb.dma('sp', xres[t4][:], x_d[tok0 + t4 * 128: tok0 + (t4 + 1) * 128, :], writes=[xres[t4]])
            ts_ = slice(tt * 128, (tt + 1) * 128)
            ypair = ((pA, pB), (pC, pD), (pA, pB), (pG, pH))[tt]
            for half, pb in enumerate(ypair):
                for kc in range(12):
                    mm(pb[:], catT[:, kc, ts_], wo[:, kc, half * 512:(half + 1) * 512], kc == 0, kc == 11, [catT, wo], [pb], kc == 11, 'op')
            yield 'B'
            cnt['ln'] += 1
            i = cnt['ln'] % 2
            rb, stt_, mv = rbuf[i], lnst[i], lnmv[i]
            xr = xres[tt]
            for half, pb in enumerate(ypair):
                hs = slice(half * 512, (half + 1) * 512)
                kb.op('dve', [xr, pb], [rb], lambda: V.scalar_tensor_tensor(out=rb[:, hs], in0=xr[:, hs], scalar=ALPHA, in1=pb[:], op0=ALU.mult, op1=ALU.add))
            ln_stats(rb, mv, stt_)
            yield 'B'
            kb.op('dve', [rb, mv, lng[l]], [rb], lambda: V.scalar_tensor_tensor(out=rb[:], in0=rb[:], scalar=mv[:, 0:1], in1=lng[l][:], op0=ALU.subtract, op1=ALU.mult))
            kb.op('dve', [rb, mv, lnb[l]], [xr], lambda: V.scalar_tensor_tensor(out=xr[:], in0=rb[:], scalar=mv[:, 2:3], in1=lnb[l][:], op0=ALU.mult, op1=ALU.add))
            if l == 0:
                to_featmajor(xr, xbb[i], xTb, tt, 'act')
            else:
                kb.dma('pool', y_d[tok0 + tt * 128: tok0 + (tt + 1) * 128, :], xr[:], reads=[xr])

        def outproj0_unit(g):
            l, tok0 = 0, g['tok0']
            flush_tail()
            for t4 in range(4):
                kb.dma('sp', xres[t4][:], x_d[tok0 + t4 * 128: tok0 + (t4 + 1) * 128, :], writes=[xres[t4]])
            pairs = ((pA, pB), (pC, pD), (pG, pH), (pA, pB))

            def mmt(tt):
                ts_ = slice(tt * 128, (tt + 1) * 128)
                for half, pb in enumerate(pairs[tt]):
                    for kc in range(12):
                        mm(pb[:], catT[:, kc, ts_], wo[:, kc, half * 512:(half + 1) * 512], kc == 0, kc == 11, [catT, wo], [pb], kc == 11, 'op')

            def a_(tt):
                i = tt % 2
                rb, stt_, mv, xr = rbuf[i], lnst[i], lnmv[i], xres[tt]
                for half, pb in enumerate(pairs[tt]):
                    hs = slice(half * 512, (half + 1) * 512)
                    kb.op('dve', [xr, pb], [rb], lambda: V.scalar_tensor_tensor(out=rb[:, hs], in0=xr[:, hs], scalar=ALPHA, in1=pb[:], op0=ALU.mult, op1=ALU.add))
                ln_stats(rb, mv, stt_)
                kb.op('dve', [rb, mv, lng[l]], [rb], lambda: V.scalar_tensor_tensor(out=rb[:], in0=rb[:], scalar=mv[:, 0:1], in1=lng[l][:], op0=ALU.subtract, op1=ALU.mult))

            def b_(tt):
                i = tt % 2
                rb, mv, xr = rbuf[i], lnmv[i], xres[tt]
                kb.op('dve', [rb, mv, lnb[l]], [xr], lambda: V.scalar_tensor_tensor(out=xr[:], in0=rb[:], scalar=mv[:, 2:3], in1=lnb[l][:], op0=ALU.mult, op1=ALU.add))
                to_featmajor(xr, xbb[i], xTb, tt, 'act')

            mmt(0); mmt(1); a_(0); mmt(2); a_(1); b_(0); mmt(3); a_(2); b_(1); a_(3); b_(2); b_(3)
            return
            yield 'B'

        subs = [(sq, st) for sq in range(NSEQ) for st in range(NSUB)]
        units = []
        gslot = 0
        for si, (sq, st) in enumerate(subs):
            tok0 = sq * S + st * ST
            nxt = subs[si + 1] if si + 1 < len(subs) else None
            for l in range(2):
                xT = xTa[si % 2] if l == 0 else xTb
                for c in range(8):
                    g = dict(l=l, c=c, sq=sq, st=st, xT=xT, gslot=gslot, next_sub=nxt)
                    units.append((hgrn_unit if l == 0 else conv_unit, g, (l == 1 and c == 0)))
                    gslot += 1
                for h in range(4):
                    g = dict(l=l, c=8, h=h, sq=sq, st=st, xT=xT, gslot=gslot, next_sub=nxt)
                    units.append((xattn_unit, g, False))
                gslot += 1
                if l == 0:
                    units.append((outproj0_unit, dict(tok0=tok0), True))
                else:
                    for tt in range(4):
                        g = dict(l=l, tt=tt, tok0=tok0)
                        units.append((outproj_unit, g, tt == 0))

        stage_x(*subs[0])

        def run_to_end(gen):
            for _ in gen:
                pass

        older = None
        for fn, g, fence in units:
            newer = fn(g)
            if fence and older is not None:
                run_to_end(older)
                older = None
            if fence:
                flush_tail()
            front_done = False
            while not front_done:
                try:
                    tag = next(newer)
                except StopIteration:
                    newer = None
                    break
                if tag == 'B':
                    front_done = True
                if older is not None:
                    try:
                        next(older)
                    except StopIteration:
                        older = None
            if older is not None:
                run_to_end(older)
            older = newer
        if older is not None:
            run_to_end(older)
        flush_tail()
        kb.barrier(['sp', 'pool'])
        print("instr counts", kb.ninst, "dma sems", kb.nbuf, "sem counts", kb.cnt)
    return nc


_NC_CACHE = {}


def _layout_w_in(w):
    wk = w.reshape(2, 8, 128, DIN)
    mix = wk[..., :4096].reshape(2, 8, 128, 4, 8, 128)
    mix = mix.transpose(0, 4, 2, 1, 3, 5)
    xa = wk[..., 4096:].reshape(2, 8, 128, 4, 128).transpose(0, 2, 1, 3, 4)[:, None]
    out = np.concatenate([mix, xa], axis=1)
    return np.ascontiguousarray(out.reshape(2, 9, 128, 4096))


def kernel(**inputs):
    x = np.ascontiguousarray(np.asarray(inputs["x"], dtype=np.float32))
    mem = np.ascontiguousarray(np.asarray(inputs["mem"], dtype=np.float32))
    Bt, S, _ = x.shape
    nseq = Bt // NCORES
    key = (nseq, S)
    if key not in _NC_CACHE:
        _NC_CACHE[key] = build(nseq, S)
    nc = _NC_CACHE[key]
    shared = {
        "w_in": _layout_w_in(np.asarray(inputs["w_in"], dtype=np.float32)),
        "w_out": np.ascontiguousarray(np.asarray(inputs["w_out"], dtype=np.float32).reshape(2, 12, 128, D).transpose(0, 2, 1, 3).reshape(2, 128, 12 * D)),
        "ln_g": np.ascontiguousarray(np.asarray(inputs["ln_g"], dtype=np.float32)),
        "ln_b": np.ascontiguousarray(np.asarray(inputs["ln_b"], dtype=np.float32)),
        "hgrn_lb_logits": np.ascontiguousarray(np.asarray(inputs["hgrn_lb_logits"], dtype=np.float32)),
        "hgrn_norm_w": np.ascontiguousarray(np.asarray(inputs["hgrn_norm_w"], dtype=np.float32)),
        "conv_w": np.ascontiguousarray(np.asarray(inputs["conv_w"], dtype=np.float32)),
        "mem_ln_g": np.ascontiguousarray(np.asarray(inputs["mem_ln_g"], dtype=np.float32)).reshape(1, D),
        "mem_ln_b": np.ascontiguousarray(np.asarray(inputs["mem_ln_b"], dtype=np.float32)).reshape(1, D),
        "w_mem_kv": np.ascontiguousarray(np.asarray(inputs["w_mem_kv"], dtype=np.float32).reshape(8, 128, D).transpose(1, 0, 2).reshape(128, 8 * D)),
    }
    in_maps = []
    for c in range(NCORES):
        m = dict(shared)
        m["x"] = x[c * nseq:(c + 1) * nseq].reshape(nseq * S, D)
        m["mem"] = mem[c * nseq:(c + 1) * nseq].reshape(nseq * NMEM, D)
        in_maps.append(m)
    res = run_bass_kernel_spmd(nc, in_maps, core_ids=list(range(NCORES)))
    out = np.concatenate([r["y"].reshape(nseq, S, D) for r in res.results], axis=0)
    return out.astype(np.float32)
```

```python
import numpy as np
from contextlib import ExitStack
import concourse.bass as bass
import concourse.mybir as mybir
from concourse.bass_utils import run_bass_kernel_spmd

F32 = mybir.dt.float32
BF16 = mybir.dt.bfloat16
AF = mybir.ActivationFunctionType
ALU = mybir.AluOpType

D = 1024
DIN = 4608
DCAT = 1536
NMEM = 256
ST = 512
ALPHA = 4.0 ** 0.25
LN_EPS = 1e-5
RMS_EPS = 1e-5
XA_SCALE = 128.0 ** -0.5
NCORES = 8


class Buf:
    def __init__(self, name, t):
        self.name = name
        self.t = t
        self.w = None
        self.r = {}
        self.pe_pend = False
        self.excl = False
        self.dkey = None
        self.dcnt = 0

    def __getitem__(self, idx):
        return self.t[idx]


class KB:
    COMPUTE = ('pe', 'act', 'dve', 'pool')

    def __init__(self, nc, es):
        self.nc = nc
        self.es = es
        self.eng = {'pe': nc.tensor, 'act': nc.scalar, 'dve': nc.vector, 'pool': nc.gpsimd, 'sp': nc.sync}
        self.sem = {}
        for e in self.COMPUTE:
            self.sem[e] = es.enter_context(nc.semaphore('s_' + e))
        self.cnt = {e: 0 for e in self.COMPUTE}
        self.waited = {e: {} for e in self.eng}
        self.pe_pending = []
        self.last = {}
        self.nbuf = 0
        self.ninst = {e: 0 for e in self.eng}

    def sb(self, name, shape, dt, es=None):
        t = (es or self.es).enter_context(self.nc.sbuf_tensor(name, list(shape), dt))
        return Buf(name, t)

    def ps(self, name, shape, dt, es=None):
        t = (es or self.es).enter_context(self.nc.psum_tensor(name, list(shape), dt))
        b = Buf(name, t)
        b.excl = True
        return b

    def dram(self, name, shape, dt):
        t = self.nc.dram_tensor(name, list(shape), dt, kind="Internal")
        return Buf(name, t)

    def _deps(self, e, reads, writes):
        deps = {}

        def add(kv):
            if kv is None:
                return
            k, v = kv
            if deps.get(k, 0) < v:
                deps[k] = v
        for b in reads:
            if b.pe_pend and e != 'pe':
                raise RuntimeError(f"buf {b.name} has unsignalled PE access (reader {e})")
            add(b.w)
            if b.excl:
                for k, v in b.r.items():
                    if k != e:
                        add((k, v))
        for b in writes:
            if b.pe_pend and e != 'pe':
                raise RuntimeError(f"buf {b.name} has unsignalled PE access (writer {e})")
            add(b.w)
            for k, v in b.r.items():
                add((k, v))
        return deps

    def _wait(self, e, deps):
        w = self.waited[e]
        for k, v in deps.items():
            if k == e and e == 'pe':
                continue
            if w.get(k, 0) < v:
                self.eng[e].wait_ge(self.sem[k], v)
                w[k] = v

    def _record(self, key, val, reads, writes):
        for b in reads:
            if b.r.get(key, 0) < val:
                b.r[key] = val
        for b in writes:
            b.w = (key, val)
            b.r = {}
        if self.last.get(key, 0) < val:
            self.last[key] = val

    def op(self, e, reads, writes, fn, sig=True):
        self._wait(e, self._deps(e, reads, writes))
        inst = fn()
        self.ninst[e] += 1
        if e == 'pe' and not sig:
            for b in reads:
                b.pe_pend = True
                self.pe_pending.append((b, 'r'))
            for b in writes:
                b.pe_pend = True
                self.pe_pending.append((b, 'w'))
            return inst
        self.cnt[e] += 1
        n = self.cnt[e]
        inst.then_inc(self.sem[e], 1)
        if e == 'pe' and self.pe_pending:
            pr = [b for b, k in self.pe_pending if k == 'r']
            pw = [b for b, k in self.pe_pending if k == 'w']
            for b, _ in self.pe_pending:
                b.pe_pend = False
            self.pe_pending = []
            self._record('pe', n, pr, pw)
        self._record(e, n, reads, writes)
        return inst

    def dma(self, q, out_ap, in_ap, reads=(), writes=(), **kw):
        owner = writes[0] if writes else reads[0]
        if owner.dkey is None:
            owner.dkey = {}
            owner.dcnt = {}
        qt = 'sw' if q == 'pool' else 'hw'
        if qt not in owner.dkey:
            owner.dkey[qt] = 'd_' + owner.name + '_' + qt
            owner.dcnt[qt] = 0
            self.sem[owner.dkey[qt]] = self.es.enter_context(self.nc.semaphore(owner.dkey[qt]))
            self.nbuf += 1
        self._wait(q, self._deps(q, list(reads), list(writes)))
        inst = self.eng[q].dma_start(out=out_ap, in_=in_ap, **kw)
        self.ninst[q] += 1
        owner.dcnt[qt] += 16
        inst.then_inc(self.sem[owner.dkey[qt]], 16)
        self._record(owner.dkey[qt], owner.dcnt[qt], list(reads), list(writes))
        return inst

    def barrier(self, engines=None):
        for e in (engines or list(self.eng)):
            self._wait(e, dict(self.last))


def build(NSEQ, S, debug=False):
    NT = NSEQ * S
    NSUB = S // ST
    nc = bass.Bass("TRN2", target_bir_lowering=False)
    x_d = nc.dram_tensor("x", [NT, D], F32, kind="ExternalInput").ap()
    mem_d = nc.dram_tensor("mem", [NSEQ * NMEM, D], F32, kind="ExternalInput").ap()
    win_d = nc.dram_tensor("w_in", [2, 9, 128, 4096], F32, kind="ExternalInput").ap()
    wout_d = nc.dram_tensor("w_out", [2, 128, 12 * D], F32, kind="ExternalInput").ap()
    lng_d = nc.dram_tensor("ln_g", [2, D], F32, kind="ExternalInput").ap()
    lnb_d = nc.dram_tensor("ln_b", [2, D], F32, kind="ExternalInput").ap()
    lbl_d = nc.dram_tensor("hgrn_lb_logits", [2, D], F32, kind="ExternalInput").ap()
    nw_d = nc.dram_tensor("hgrn_norm_w", [1, 128], F32, kind="ExternalInput").ap()
    cw_d = nc.dram_tensor("conv_w", [1, 3, D], F32, kind="ExternalInput").ap()
    mg_d = nc.dram_tensor("mem_ln_g", [1, D], F32, kind="ExternalInput").ap()
    mb_d = nc.dram_tensor("mem_ln_b", [1, D], F32, kind="ExternalInput").ap()
    wkv_d = nc.dram_tensor("w_mem_kv", [128, 8 * D], F32, kind="ExternalInput").ap()
    y_d = nc.dram_tensor("y", [NT, D], F32, kind="ExternalOutput").ap()
    dbg = {}
    if debug:
        dbg['kt'] = nc.dram_tensor("dbg_kt", [128, NSEQ * 4 * 256], F32, kind="ExternalOutput").ap()
        dbg['v'] = nc.dram_tensor("dbg_v", [128, NSEQ * 2 * 512], F32, kind="ExternalOutput").ap()
        dbg['win'] = nc.dram_tensor("dbg_win", [128, 4096], F32, kind="ExternalOutput").ap()

    with ExitStack() as es:
        kb = KB(nc, es)
        V, A, P, T = kb.eng['dve'], kb.eng['act'], kb.eng['pool'], kb.eng['pe']

        winB = kb.dram("winB", [2, 9, 128, 4096], BF16)
        woutB = kb.dram("woutB", [2, 128, 12 * D], BF16)

        ident = kb.sb("ident", [128, 128], BF16)
        ones = kb.sb("ones", [128, 128], BF16)
        cmask = kb.sb("cmask", [128, 128], F32)
        lng = [kb.sb(f"lng{l}", [128, D], F32) for l in range(2)]
        lnb = [kb.sb(f"lnb{l}", [128, D], F32) for l in range(2)]
        KT = kb.sb("KT", [128, NSEQ, 4, 256], BF16)
        VV = kb.sb("VV", [128, NSEQ, 2, 512], BF16)
        lbp = kb.sb("lbp", [128, 3, 8], F32)
        convw = kb.sb("convw", [128, 3, 8], F32)
        normw = kb.sb("normw", [128, 2], F32)
        mhalf = kb.sb("mhalf", [128, 1], F32)
        epsln = kb.sb("epsln", [128, 1], F32)

        with ExitStack() as pes:
            tmpf = kb.sb("tmpf", [128, 128], F32, pes)
            kb.op('pool', [], [tmpf], lambda: P.memset(tmpf[:], 1.0))
            kb.op('pool', [tmpf], [cmask], lambda: P.affine_select(
                out=cmask[:], in_=tmpf[:], pattern=[[1, 128]], compare_op=ALU.is_ge, fill=0.0,
                base=0, channel_multiplier=-1))
            identf = kb.sb("identf", [128, 128], F32, pes)
            kb.op('pool', [tmpf], [identf], lambda: P.affine_select(
                out=identf[:], in_=tmpf[:], pattern=[[-1, 128]], compare_op=ALU.is_equal, fill=0.0,
                base=0, channel_multiplier=1))
            kb.op('pool', [identf], [ident], lambda: P.tensor_copy(ident[:], identf[:]))
            kb.op('pool', [tmpf], [ones], lambda: P.tensor_copy(ones[:], tmpf[:]))
            kb.op('pool', [], [mhalf], lambda: P.memset(mhalf[:], -0.5))
            kb.op('pool', [], [epsln], lambda: P.memset(epsln[:], LN_EPS))

            for l in range(2):
                kb.dma('sp', lng[l][:], lng_d[l:l + 1, :].partition_broadcast(128), writes=[lng[l]])
                kb.dma('sp', lnb[l][:], lnb_d[l:l + 1, :].partition_broadcast(128), writes=[lnb[l]])
            mlg = kb.sb("mlg", [128, D], F32, pes)
            mlb = kb.sb("mlb", [128, D], F32, pes)
            kb.dma('sp', mlg[:], mg_d[0:1, :].partition_broadcast(128), writes=[mlg])
            kb.dma('sp', mlb[:], mb_d[0:1, :].partition_broadcast(128), writes=[mlb])
            lbl = kb.sb("lbl", [128, 2, 8], F32, pes)
            with nc.allow_non_contiguous_dma(reason="tiny param gathers"):
                kb.dma('sp', lbl[:], lbl_d.rearrange("r (h p) -> p r h", p=128), writes=[lbl])
                kb.dma('sp', convw[:], cw_d[0].rearrange("w (c p) -> p w c", p=128), writes=[convw])
                kb.dma('sp', normw[:, 0:1], nw_d.rearrange("o p -> p o"), writes=[normw])
            kb.op('dve', [lbl], [lbl], lambda: V.tensor_tensor(out=lbl[:, 1, :], in0=lbl[:, 0, :], in1=lbl[:, 1, :], op=ALU.subtract))
            kb.op('act', [lbl], [lbp], lambda: A.activation(out=lbp[:, 0, :], in_=lbl[:, 1, :], func=AF.Sigmoid))
            kb.op('dve', [lbp], [lbp], lambda: V.tensor_scalar(out=lbp[:, 0, :], in0=lbp[:, 0, :], scalar1=-0.5, scalar2=0.5, op0=ALU.mult, op1=ALU.add))
            kb.op('dve', [lbp], [lbp], lambda: V.tensor_scalar(out=lbp[:, 1, :], in0=lbp[:, 0, :], scalar1=-1.0, scalar2=1.0, op0=ALU.mult, op1=ALU.add))
            kb.op('dve', [lbp], [lbp], lambda: V.tensor_scalar(out=lbp[:, 2, :], in0=lbp[:, 0, :], scalar1=-1.0, scalar2=None, op0=ALU.mult))
            kb.op('dve', [normw], [normw], lambda: V.tensor_scalar(out=normw[:, 1:2], in0=normw[:, 0:1], scalar1=1.0, scalar2=None, op0=ALU.mult))

            wkv = kb.sb("wkv", [128, 8, D], BF16, pes)
            kb.dma('pool', wkv[:].rearrange("p k f -> p (k f)"), wkv_d[:, :], writes=[wkv])

            mt = [kb.sb(f"mt{i}", [128, D], F32, pes) for i in range(2)]
            mtb = [kb.sb(f"mtb{i}", [128, D], BF16, pes) for i in range(2)]
            mstat = kb.sb("mstat", [128, 2, 6], F32, pes)
            mmv = kb.sb("mmv", [128, 4], F32, pes)
            memT = kb.sb("memT", [128, 8, 256], BF16, pes)
            pt_tr = kb.ps("pp_tr", [128, 8, 128], BF16, pes)
            pt_a = kb.ps("pp_a", [128, 512], F32, pes)
            for sq in range(NSEQ):
                for mc in range(2):
                    i = (sq * 2 + mc) % 2
                    m, mb_ = mt[i], mtb[i]
                    kb.dma('sp', m[:], mem_d[sq * 256 + mc * 128: sq * 256 + (mc + 1) * 128, :], writes=[m])
                    for hf in range(2):
                        kb.op('dve', [m], [mstat], lambda hf=hf: V.bn_stats(out=mstat[:, hf, :], in_=m[:, hf * 512:(hf + 1) * 512]))
                    kb.op('dve', [mstat], [mmv], lambda: V.bn_aggr(out=mmv[:, 0:2], in_=mstat[:].rearrange("p a b -> p (a b)")))
                    kb.op('dve', [mmv], [mmv], lambda: V.tensor_scalar(out=mmv[:, 1:2], in0=mmv[:, 1:2], scalar1=LN_EPS, scalar2=None, op0=ALU.add))
                    kb.op('pool', [mmv, mhalf], [mmv], lambda: P.tensor_tensor(out=mmv[:, 2:3], in0=mmv[:, 1:2], in1=mhalf[:], op=ALU.pow))
                    kb.op('dve', [mmv], [mmv], lambda: V.scalar_tensor_tensor(out=mmv[:, 3:4], in0=mmv[:, 0:1], scalar=-1.0, in1=mmv[:, 2:3], op0=ALU.mult, op1=ALU.mult))
                    kb.op('act', [m, mmv], [m], lambda: A.activation(out=m[:], in_=m[:], func=AF.Identity, scale=mmv[:, 2:3], bias=mmv[:, 3:4]))
                    kb.op('dve', [m, mlg], [m], lambda: V.tensor_tensor(out=m[:], in0=m[:], in1=mlg[:], op=ALU.mult))
                    kb.op('dve', [m, mlb], [mb_], lambda: V.tensor_tensor(out=mb_[:], in0=m[:], in1=mlb[:], op=ALU.add))
                    for kc in range(8):
                        kb.op('pe', [mb_, ident], [pt_tr], lambda kc=kc: T.transpose(out=pt_tr[:, kc, :], in_=mb_[:, kc * 128:(kc + 1) * 128], identity=ident[:]), sig=(kc == 7))
                    kb.op('act', [pt_tr], [memT], lambda: A.copy(out=memT[:, :, mc * 128:(mc + 1) * 128], in_=pt_tr[:]))
                for h in range(4):
                    for kc in range(8):
                        kb.op('pe', [wkv, memT], [pt_a], lambda h=h, kc=kc: T.matmul(pt_a[:, 0:256], lhsT=wkv[:, kc, h * 128:(h + 1) * 128], rhs=memT[:, kc, :], start=(kc == 0), stop=(kc == 7)), sig=(kc == 7))
                    kb.op('dve', [pt_a], [KT], lambda h=h: V.tensor_copy(KT[:, sq, h, :], pt_a[:, 0:256]))
                for mc in range(2):
                    for kc in range(8):
                        kb.op('pe', [wkv, memT], [pt_a], lambda mc=mc, kc=kc: T.matmul(pt_a[:], lhsT=memT[:, kc, mc * 128:(mc + 1) * 128], rhs=wkv[:, kc, 512:1024], start=(kc == 0), stop=(kc == 7)), sig=(kc == 7))
                    kb.op('act', [pt_a], [VV], lambda mc=mc: A.copy(out=VV[:, sq, mc, :], in_=pt_a[:]))

            kb.barrier()

        rmseps = kb.sb("rmseps", [128, 1], F32)
        kb.op('pool', [], [rmseps], lambda: P.memset(rmseps[:], RMS_EPS))
        xres = [kb.sb(f"xres{i}", [128, D], F32) for i in range(4)]
        xin = kb.sb("xin", [128, D], F32)
        xbb = [kb.sb(f"xbb{i}", [128, D], BF16) for i in range(2)]
        xsb = [kb.sb(f"xsb{i}", [128, D], BF16) for i in range(4)]
        rbuf = [kb.sb(f"rbuf{i}", [128, D], F32) for i in range(2)]
        xTa = [kb.sb(f"xTa{i}", [128, 8, ST], BF16) for i in range(2)]
        xTb = kb.sb("xTb", [128, 8, ST], BF16)
        catT = kb.sb("catT", [128, 12, ST], BF16)
        wring = [kb.sb(f"wring{i}", [128, 8, 4, 128], BF16) for i in range(3)]
        wo = kb.sb("wo", [128, 12, D], BF16)
        FT = [kb.sb(f"ft{i}", [128, ST + 4], F32) for i in range(12)]
        BT = [kb.sb(f"bt{i}", [128, ST], BF16) for i in range(14)]
        Wst = [[kb.sb(f"W{h}_{i}", [128, 128], F32) for i in range(2)] for h in range(8)]
        bcar = kb.sb("bcar", [128, 8], F32)
        rho = kb.sb("rho", [128, 8, 5], F32)
        dgm = [kb.sb(f"dgm{i}", [128, 8], F32) for i in range(2)]
        halo = kb.sb("halo", [128, 8, 2], F32)
        lnst = [kb.sb(f"lnst{i}", [128, 2, 6], F32) for i in range(2)]
        lnmv = [kb.sb(f"lnmv{i}", [128, 6], F32) for i in range(2)]
        pbank = [kb.ps(f"pb{i}", [128, ST], F32) for i in range(8)]
        pA, pB, pC, pD, pE, pF, pG, pH = pbank
        cnt = {'ln': 0, 'u': 0}
        onesb = ones[:, 0:1].to_broadcast([128, ST])

        slot_seq = [(l, c) for _sq in range(NSEQ) for _st in range(NSUB) for l in range(2) for c in range(9)]
        slot_issued = [0]

        def issue_slots(upto):
            while slot_issued[0] < min(upto, len(slot_seq)):
                i = slot_issued[0]
                l, c = slot_seq[i]
                wb = wring[i % 3]
                if i < 18:
                    kb.dma('pool', wb[:].rearrange("p k a j -> p (k a j)"), win_d[l, c, :, :], writes=[wb])
                    kb.dma('sp', winB.t.ap()[l, c, :, :], wb[:].rearrange("p k a j -> p (k a j)"), reads=[wb], writes=[winB])
                else:
                    kb.dma('sp', wb[:].rearrange("p k a j -> p (k a j)"), winB.t.ap()[l, c, :, :], reads=[winB], writes=[wb])
                slot_issued[0] += 1

        def mm(out_ap, lhsT, rhs, start, stop, reads, writes, sig, tag=None):
            if tag == 'sc':
                kb.op('pe', reads, writes, lambda: T.matmul(out_ap, lhsT=lhsT, rhs=rhs, start=start, stop=stop), sig=sig)
            elif tag == 'ds':
                kb.op('pe', reads, writes, lambda: T.matmul(out_ap, lhsT=lhsT, rhs=rhs, start=start, stop=stop), sig=sig)
            elif tag == 'o1':
                kb.op('pe', reads, writes, lambda: T.matmul(out_ap, lhsT=lhsT, rhs=rhs, start=start, stop=stop), sig=sig)
            elif tag == 'o2':
                kb.op('pe', reads, writes, lambda: T.matmul(out_ap, lhsT=lhsT, rhs=rhs, start=start, stop=stop), sig=sig)
            elif tag == 'v':
                kb.op('pe', reads, writes, lambda: T.matmul(out_ap, lhsT=lhsT, rhs=rhs, start=start, stop=stop), sig=sig)
            elif tag == 'ss':
                kb.op('pe', reads, writes, lambda: T.matmul(out_ap, lhsT=lhsT, rhs=rhs, start=start, stop=stop), sig=sig)
            elif tag == 'xa':
                kb.op('pe', reads, writes, lambda: T.matmul(out_ap, lhsT=lhsT, rhs=rhs, start=start, stop=stop), sig=sig)
            elif tag == 'op':
                kb.op('pe', reads, writes, lambda: T.matmul(out_ap, lhsT=lhsT, rhs=rhs, start=start, stop=stop), sig=sig)
            else:
                kb.op('pe', reads, writes, lambda: T.matmul(out_ap, lhsT=lhsT, rhs=rhs, start=start, stop=stop), sig=sig)

        def proj_fm(pb, wb, part, xT):
            for kc in range(8):
                mm(pb[:], wb[:, kc, part, :], xT[:, kc, :], kc == 0, kc == 7, [wb, xT], [pb], kc == 7)

        def to_featmajor(src, xb, xT, tt, cast_eng):
            if cast_eng == 'act':
                kb.op('act', [src], [xb], lambda: A.copy(out=xb[:], in_=src[:]))
            else:
                kb.op('pool', [src], [xb], lambda: P.tensor_copy(xb[:], src[:]))
            pEb = pE[:].bitcast(BF16).rearrange("p (k t) -> p k t", k=8)
            for kc in range(8):
                kb.op('pe', [xb, ident], [pE], lambda kc=kc: T.transpose(out=pEb[:, kc, :], in_=xb[:, kc * 128:(kc + 1) * 128], identity=ident[:]), sig=(kc == 7))
            if cast_eng == 'act':
                kb.op('act', [pE], [xT], lambda: A.copy(out=xT[:, :, tt * 128:(tt + 1) * 128], in_=pEb))
            else:
                kb.op('dve', [pE], [xT], lambda: V.tensor_copy(xT[:, :, tt * 128:(tt + 1) * 128], pEb))

        def ln_stats(rb, mv, st_):
            for half in range(2):
                kb.op('dve', [rb], [st_], lambda: V.bn_stats(out=st_[:, half, :], in_=rb[:, half * 512:(half + 1) * 512]))
            kb.op('dve', [st_], [mv], lambda: V.bn_aggr(out=mv[:, 0:2], in_=st_[:].rearrange("p a b -> p (a b)")))
            kb.op('dve', [mv], [mv], lambda: V.tensor_scalar(out=mv[:, 1:2], in0=mv[:, 1:2], scalar1=LN_EPS, scalar2=None, op0=ALU.add))
            kb.op('pool', [mv, mhalf], [mv], lambda: P.tensor_tensor(out=mv[:, 2:3], in0=mv[:, 1:2], in1=mhalf[:], op=ALU.pow))

        def stage_x_load(sq, st, tt):
            tok0 = sq * S + st * ST
            kb.dma('pool', xsb[tt][:], x_d[tok0 + tt * 128: tok0 + (tt + 1) * 128, :], writes=[xsb[tt]])

        def stage_x_tr(sq, st, tt):
            xT = xTa[(sq * NSUB + st) % 2]
            xb = xsb[tt]
            pEb = pE[:].bitcast(BF16).rearrange("p (k t) -> p k t", k=8)
            for kc in range(8):
                kb.op('pe', [xb, ident], [pE], lambda kc=kc: T.transpose(out=pEb[:, kc, :], in_=xb[:, kc * 128:(kc + 1) * 128], identity=ident[:]), sig=(kc == 7))
            kb.op('dve', [pE], [xT], lambda: V.tensor_copy(xT[:, :, tt * 128:(tt + 1) * 128], pEb))

        def stage_x(sq, st):
            for tt in range(4):
                stage_x_load(sq, st, tt)
                stage_x_tr(sq, st, tt)

        def unit_prologue(g):
            issue_slots(g['gslot'] + 3)
            if g['c'] < 4:
                l, q4 = g['l'], g['c']
                wo_v = wo[:, q4 * 3:(q4 + 1) * 3, :].rearrange("p k f -> p (k f)")
                if g['gslot'] < 18:
                    kb.dma('pool', wo_v, wout_d[l, :, q4 * 3 * D:(q4 + 1) * 3 * D], writes=[wo])
                    kb.dma('sp', woutB.t.ap()[l, :, q4 * 3 * D:(q4 + 1) * 3 * D], wo_v, reads=[wo], writes=[woutB])
                else:
                    kb.dma('sp', wo_v, woutB.t.ap()[l, :, q4 * 3 * D:(q4 + 1) * 3 * D], reads=[woutB], writes=[wo])
            if g['l'] == 1 and g['next_sub'] is not None and g['c'] < 8:
                if g['c'] < 4:
                    stage_x_load(*g['next_sub'], g['c'])
                else:
                    stage_x_tr(*g['next_sub'], g['c'] - 4)
            if g['l'] == 0 and g['c'] == 0 and g['st'] == 0:
                for h in range(8):
                    kb.op('pool', [], [Wst[h][0]], lambda: P.memset(Wst[h][0][:], 0.0))
                kb.op('pool', [], [bcar], lambda: P.memset(bcar[:], 0.0))
                kb.op('pool', [], [rho], lambda: P.memset(rho[:], 0.0))
                kb.op('pool', [], [halo], lambda: P.memset(halo[:], 0.0))

        tail = {}

        def flush_tail(which=None):
            order = ['o', 'ss', 'rms']
            upto = 2 if which is None else order.index(which)
            for k in order[:upto + 1]:
                if k in tail:
                    tail.pop(k)()

        def hgrn_unit(g):
            h, xT = g['c'], g['xT']
            wb = wring[g['gslot'] % 3]
            cnt['u'] += 1
            par = cnt['u'] % 2
            qf, thf, lf, Bt, Dt, En = FT[0:6]
            gf = (FT[6], FT[7], FT[11])[cnt['u'] % 3]
            rs, t1 = FT[8], FT[9]
            vb, qtb, ktb = BT[0 + par], BT[2 + par], BT[4 + par]
            ktok, Ab, Ub, sqb = BT[6], BT[7], BT[8], BT[9]
            dg = dgm[par]
            unit_prologue(g)
            proj_fm(pB, wb, 1, xT)
            kb.op('act', [pB], [thf], lambda: A.activation(out=thf[:, 0:ST], in_=pB[:], func=AF.Tanh, scale=0.5))
            flush_tail('o')
            yield 'F'
            proj_fm(pC, wb, 3, xT)
            kb.op('act', [pC], [gf], lambda: A.activation(out=gf[:, 0:ST], in_=pC[:], func=AF.Silu))
            yield 'F'
            proj_fm(pA, wb, 0, xT)
            kb.op('act', [pA], [qf], lambda: A.activation(out=qf[:, 0:ST], in_=pA[:], func=AF.Silu))
            yield 'F'
            kb.op('act', [thf, lbp], [lf], lambda: A.activation(out=lf[:, 0:ST], in_=thf[:, 0:ST], func=AF.Ln, scale=lbp[:, 0, h:h + 1], bias=lbp[:, 1, h:h + 1]))
            kb.op('dve', [lf, ones, bcar], [Bt], lambda: V.tensor_tensor_scan(out=Bt[:, 0:ST], data0=onesb, data1=lf[:, 0:ST], initial=bcar[:, h:h + 1], op0=ALU.mult, op1=ALU.add))
            B3 = Bt[:, 0:ST].rearrange("p (c t) -> p c t", c=4)
            kb.op('dve', [Bt], [rho], lambda: V.tensor_copy(rho[:, h, 1:5], B3[:, :, 64]))
            kb.op('dve', [Bt, rho], [Dt], lambda: V.tensor_tensor(out=Dt[:, 0:ST].rearrange("p (c t) -> p c t", c=4), in0=B3,
                                                                   in1=rho[:, h, 1:5].unsqueeze(2).to_broadcast([128, 4, 128]), op=ALU.subtract))
            flush_tail('rms')
            kb.op('act', [Dt], [En], lambda: A.activation(out=En[:, 0:ST], in_=Dt[:, 0:ST], func=AF.Exp, scale=-1.0))
            kb.op('act', [Dt], [Dt], lambda: A.activation(out=Dt[:, 0:ST], in_=Dt[:, 0:ST], func=AF.Exp))
            kb.op('dve', [Bt], [bcar], lambda: V.tensor_copy(bcar[:, h:h + 1], Bt[:, ST - 1:ST]))
            kb.op('dve', [rho], [dg], lambda: V.tensor_tensor(out=dg[:, 0:4], in0=rho[:, h, 1:5], in1=rho[:, h, 0:4], op=ALU.subtract))
            kb.op('dve', [rho], [rho], lambda: V.tensor_copy(rho[:, h, 0:1], rho[:, h, 4:5]))
            kb.op('act', [dg], [dg], lambda: A.activation(out=dg[:, 4:8], in_=dg[:, 0:4], func=AF.Exp))
            kb.op('dve', [thf, En], [ktb], lambda: V.scalar_tensor_tensor(out=ktb[:], in0=thf[:, 0:ST], scalar=-1.0, in1=En[:, 0:ST], op0=ALU.add, op1=ALU.mult))
            kb.op('dve', [qf, Dt, lbp], [qtb], lambda: V.scalar_tensor_tensor(out=qtb[:], in0=qf[:, 0:ST], scalar=lbp[:, 2, h:h + 1], in1=Dt[:, 0:ST], op0=ALU.mult, op1=ALU.mult))
            yield 'F'
            for c in range(4):
                for kc in range(8):
                    mm(pD[:, c * 128:(c + 1) * 128], xT[:, kc, c * 128:(c + 1) * 128], wb[:, kc, 2, :], kc == 0, kc == 7,
                       [wb, xT], [pD], kc == 7 and c == 3, 'v')
            kb.op('act', [pD], [vb], lambda: A.copy(out=vb[:], in_=pD[:]))
            yield 'B'
            yield 'B'
            yield 'B'
            pEb = pE[:].bitcast(BF16)
            for c in range(4):
                kb.op('pe', [ktb, ident], [pE], lambda c=c: T.transpose(out=pEb[:, c * 128:(c + 1) * 128], in_=ktb[:, c * 128:(c + 1) * 128], identity=ident[:]), sig=(c == 3))
            kb.op('dve', [pE], [ktok], lambda: V.tensor_copy(ktok[:], pEb[:, 0:ST]))
            for c in range(4):
                cs = slice(c * 128, (c + 1) * 128)
                mm(pF[:, cs], ktb[:, cs], qtb[:, cs], True, True, [ktb, qtb], [pF], c == 3, 'sc')
            kb.op('dve', [pF, cmask], [Ab], lambda: V.tensor_tensor(out=Ab[:].rearrange("p (c t) -> p c t", c=4), in0=pF[:].rearrange("p (c t) -> p c t", c=4),
                                                                    in1=cmask[:].unsqueeze(1).to_broadcast([128, 4, 128]), op=ALU.mult))
            for c in range(4):
                cs = slice(c * 128, (c + 1) * 128)
                mm(pG[:, cs], ktok[:, cs], vb[:, cs], True, True, [ktok, vb], [pG], True, 'ds')
            for c in range(4):
                cs = slice(c * 128, (c + 1) * 128)
                wp, wn = Wst[h][c % 2], Wst[h][(c + 1) % 2]
                kb.op('dve', [wp, dg], [Ub], lambda: V.tensor_scalar(out=Ub[:, cs], in0=wp[:], scalar1=dg[:, 4 + c:5 + c], scalar2=None, op0=ALU.mult))
                kb.op('dve', [wp, dg, pG], [wn], lambda: V.scalar_tensor_tensor(out=wn[:], in0=wp[:], scalar=dg[:, 4 + c:5 + c], in1=pG[:, cs], op0=ALU.mult, op1=ALU.add))
            yield 'B'
            def tail_o():
                for c in range(4):
                    cs = slice(c * 128, (c + 1) * 128)
                    mm(pH[:, cs], vb[:, cs], Ab[:, cs], True, False, [vb, Ab], [pH], False, 'o1')
                    mm(pH[:, cs], Ub[:, cs], qtb[:, cs], False, True, [Ub, qtb], [pH], c == 3, 'o2')
                kb.op('act', [pH], [sqb], lambda: A.activation(out=sqb[:], in_=pH[:], func=AF.Square))
                kb.op('act', [pH, normw], [t1], lambda: A.activation(out=t1[:, 0:ST], in_=pH[:], func=AF.Copy, scale=normw[:, 1:2]))
                tail['ss'] = tail_ss
                tail['rms'] = tail_rms

            def tail_ss():
                mm(pH[:], ones[:], sqb[:], True, True, [ones, sqb], [pH], True, 'ss')
                kb.op('act', [pH], [rs], lambda: A.copy(out=rs[:, 0:ST], in_=pH[:]))

            def tail_rms():
                kb.op('act', [rs, rmseps], [rs], lambda: A.activation(out=rs[:, 0:ST], in_=rs[:, 0:ST], func=AF.Ln, scale=1.0 / 128.0, bias=rmseps[:, 0:1]))
                kb.op('act', [rs], [rs], lambda: A.activation(out=rs[:, 0:ST], in_=rs[:, 0:ST], func=AF.Exp, scale=-0.5))
                kb.op('dve', [t1, rs], [t1], lambda: V.tensor_tensor(out=t1[:, 0:ST], in0=t1[:, 0:ST], in1=rs[:, 0:ST], op=ALU.mult))
                kb.op('dve', [t1, gf], [catT], lambda: V.tensor_tensor(out=catT[:, h, :], in0=t1[:, 0:ST], in1=gf[:, 0:ST], op=ALU.mult))
            assert not tail
            tail['o'] = tail_o

        def conv_unit(g):
            c, xT = g['c'], g['xT']
            wb = wring[g['gslot'] % 3]
            cnt['u'] += 1
            par = cnt['u'] % 2
            zaf, bcp, ub, sg = FT[0 + par], FT[2 + par], FT[4 + par], FT[6 + par]
            cv, m = FT[8], FT[9]
            flush_tail()
            unit_prologue(g)
            proj_fm(pA, wb, 0, xT)
            kb.op('act', [pA], [zaf], lambda: A.copy(out=zaf[:, 0:ST], in_=pA[:]))
            yield 'F'
            proj_fm(pB, wb, 1, xT)
            kb.op('act', [pB], [bcp], lambda: A.copy(out=bcp[:, 0:ST], in_=pB[:]))
            yield 'F'
            proj_fm(pC, wb, 2, xT)
            kb.op('pool', [halo], [ub], lambda: P.tensor_copy(ub[:, 0:2], halo[:, c, :]))
            kb.op('dve', [pC, bcp], [ub], lambda: V.tensor_tensor(out=ub[:, 2:ST + 2], in0=pC[:], in1=bcp[:, 0:ST], op=ALU.mult))
            kb.op('pool', [ub], [halo], lambda: P.tensor_copy(halo[:, c, :], ub[:, ST:ST + 2]))
            yield 'F'
            proj_fm(pD, wb, 3, xT)
            kb.op('act', [pD], [sg], lambda: A.activation(out=sg[:, 0:ST], in_=pD[:], func=AF.Silu))
            yield 'B'
            kb.op('act', [ub, convw], [cv], lambda: A.activation(out=cv[:, 0:ST], in_=ub[:, 2:ST + 2], func=AF.Copy, scale=convw[:, 2, c:c + 1]))
            kb.op('dve', [ub, convw, cv], [cv], lambda: V.scalar_tensor_tensor(out=cv[:, 0:ST], in0=ub[:, 1:ST + 1], scalar=convw[:, 1, c:c + 1], in1=cv[:, 0:ST], op0=ALU.mult, op1=ALU.add))
            yield 'B'
            kb.op('dve', [ub, convw, cv], [cv], lambda: V.scalar_tensor_tensor(out=cv[:, 0:ST], in0=ub[:, 0:ST], scalar=convw[:, 0, c:c + 1], in1=cv[:, 0:ST], op0=ALU.mult, op1=ALU.add))
            kb.op('pool', [zaf, sg], [m], lambda: P.tensor_tensor(out=m[:, 0:ST], in0=zaf[:, 0:ST], in1=sg[:, 0:ST], op=ALU.mult))
            yield 'B'
            kb.op('dve', [m, cv], [catT], lambda: V.tensor_tensor(out=catT[:, c, :], in0=m[:, 0:ST], in1=cv[:, 0:ST], op=ALU.mult))

        def xattn_unit(g):
            h, xT, sq = g['h'], g['xT'], g['sq']
            wb = wring[g['gslot'] % 3]
            cnt['u'] += 1
            par = cnt['u'] % 2
            qx, E0, E1, rden = BT[10 + par], BT[12], BT[13], FT[10]
            flush_tail('o')
            if h == 0:
                unit_prologue(g)
            proj_fm(pA, wb, h, xT)
            kb.op('act', [pA], [qx], lambda: A.copy(out=qx[:], in_=pA[:]))
            yield 'F'
            flush_tail()
            for mc, (pb, Eb) in enumerate(((pB, E0), (pC, E1))):
                mm(pb[:], KT[:, sq, h, mc * 128:(mc + 1) * 128], qx[:], True, True, [KT, qx], [pb], True)
                kb.op('act', [pb], [Eb], lambda: A.activation(out=Eb[:], in_=pb[:], func=AF.Exp, scale=XA_SCALE))
            yield 'B'
            for mc, Eb in enumerate((E0, E1)):
                mm(pD[:], VV[:, sq, mc, h * 128:(h + 1) * 128], Eb[:], mc == 0, mc == 1, [VV, Eb], [pD], mc == 1)
            for mc, Eb in enumerate((E0, E1)):
                mm(pF[:], ones[:], Eb[:], mc == 0, mc == 1, [ones, Eb], [pF], mc == 1)
            kb.op('act', [pF], [rden], lambda: A.activation(out=rden[:, 0:ST], in_=pF[:], func=AF.Ln))
            kb.op('act', [rden], [rden], lambda: A.activation(out=rden[:, 0:ST], in_=rden[:, 0:ST], func=AF.Exp, scale=-1.0))
            kb.op('dve', [pD, rden], [catT], lambda: V.tensor_tensor(out=catT[:, 8 + h, :], in0=pD[:], in1=rden[:, 0:ST], op=ALU.mult))

        def outproj_unit(g):
            l, tt, tok0 = g['l'], g['tt'], g['tok0']
            flush_tail()
            if l == 0 and tt == 0:
                for t4 in range(4):
                    kb.dma('sp', xres[t4][:], x_d[tok0 + t4 * 128: tok0 + (t4 + 1) * 128, :], writes=[xres[t4]])
            ts_ = slice(tt * 128, (tt + 1) * 128)
            ypair = ((pA, pB), (pC, pD), (pA, pB), (pG, pH))[tt]
            for half, pb in enumerate(ypair):
                for kc in range(12):
                    mm(pb[:], catT[:, kc, ts_], wo[:, kc, half * 512:(half + 1) * 512], kc == 0, kc == 11, [catT, wo], [pb], kc == 11, 'op')
            yield 'B'
            cnt['ln'] += 1
            i = cnt['ln'] % 2
            rb, stt_, mv = rbuf[i], lnst[i], lnmv[i]
            xr = xres[tt]
            for half, pb in enumerate(ypair):
                hs = slice(half * 512, (half + 1) * 512)
                kb.op('dve', [xr, pb], [rb], lambda: V.scalar_tensor_tensor(out=rb[:, hs], in0=xr[:, hs], scalar=ALPHA, in1=pb[:], op0=ALU.mult, op1=ALU.add))
            ln_stats(rb, mv, stt_)
            yield 'B'
            kb.op('dve', [rb, mv, lng[l]], [rb], lambda: V.scalar_tensor_tensor(out=rb[:], in0=rb[:], scalar=mv[:, 0:1], in1=lng[l][:], op0=ALU.subtract, op1=ALU.mult))
            kb.op('dve', [rb, mv, lnb[l]], [xr], lambda: V.scalar_tensor_tensor(out=xr[:], in0=rb[:], scalar=mv[:, 2:3], in1=lnb[l][:], op0=ALU.mult, op1=ALU.add))
            if l == 0:
                to_featmajor(xr, xbb[i], xTb, tt, 'act')
            else:
                kb.dma('pool', y_d[tok0 + tt * 128: tok0 + (tt + 1) * 128, :], xr[:], reads=[xr])

        def outproj0_unit(g):
            l, tok0 = 0, g['tok0']
            flush_tail()
            for t4 in range(4):
                kb.dma('sp', xres[t4][:], x_d[tok0 + t4 * 128: tok0 + (t4 + 1) * 128, :], writes=[xres[t4]])
            pairs = ((pA, pB), (pC, pD), (pG, pH), (pA, pB))

            def mmt(tt):
                ts_ = slice(tt * 128, (tt + 1) * 128)
                for half, pb in enumerate(pairs[tt]):
                    for kc in range(12):
                        mm(pb[:], catT[:, kc, ts_], wo[:, kc, half * 512:(half + 1) * 512], kc == 0, kc == 11, [catT, wo], [pb], kc == 11, 'op')

            def a_(tt):
                i = tt % 2
                rb, stt_, mv, xr = rbuf[i], lnst[i], lnmv[i], xres[tt]
                for half, pb in enumerate(pairs[tt]):
                    hs = slice(half * 512, (half + 1) * 512)
                    kb.op('dve', [xr, pb], [rb], lambda: V.scalar_tensor_tensor(out=rb[:, hs], in0=xr[:, hs], scalar=ALPHA, in1=pb[:], op0=ALU.mult, op1=ALU.add))
                ln_stats(rb, mv, stt_)
                kb.op('dve', [rb, mv, lng[l]], [rb], lambda: V.scalar_tensor_tensor(out=rb[:], in0=rb[:], scalar=mv[:, 0:1], in1=lng[l][:], op0=ALU.subtract, op1=ALU.mult))

            def b_(tt):
                i = tt % 2
                rb, mv, xr = rbuf[i], lnmv[i], xres[tt]
                kb.op('dve', [rb, mv, lnb[l]], [xr], lambda: V.scalar_tensor_tensor(out=xr[:], in0=rb[:], scalar=mv[:, 2:3], in1=lnb[l][:], op0=ALU.mult, op1=ALU.add))
                to_featmajor(xr, xbb[i], xTb, tt, 'act')

            mmt(0); mmt(1); a_(0); mmt(2); a_(1); b_(0); mmt(3); a_(2); b_(1); a_(3); b_(2); b_(3)
            return
            yield 'B'

        subs = [(sq, st) for sq in range(NSEQ) for st in range(NSUB)]
        units = []
        gslot = 0
        for si, (sq, st) in enumerate(subs):
            tok0 = sq * S + st * ST
            nxt = subs[si + 1] if si + 1 < len(subs) else None
            for l in range(2):
                xT = xTa[si % 2] if l == 0 else xTb
                for c in range(8):
                    g = dict(l=l, c=c, sq=sq, st=st, xT=xT, gslot=gslot, next_sub=nxt)
                    units.append((hgrn_unit if l == 0 else conv_unit, g, (l == 1 and c == 0)))
                    gslot += 1
                for h in range(4):
                    g = dict(l=l, c=8, h=h, sq=sq, st=st, xT=xT, gslot=gslot, next_sub=nxt)
                    units.append((xattn_unit, g, False))
                gslot += 1
                if l == 0:
                    units.append((outproj0_unit, dict(tok0=tok0), True))
                else:
                    for tt in range(4):
                        g = dict(l=l, tt=tt, tok0=tok0)
                        units.append((outproj_unit, g, tt == 0))

        stage_x(*subs[0])

        def run_to_end(gen):
            for _ in gen:
                pass

        older = None
        for fn, g, fence in units:
            newer = fn(g)
            if fence and older is not None:
                run_to_end(older)
                older = None
            if fence:
                flush_tail()
            front_done = False
            while not front_done:
                try:
                    tag = next(newer)
                except StopIteration:
                    newer = None
                    break
                if tag == 'B':
                    front_done = True
                if older is not None:
                    try:
                        next(older)
                    except StopIteration:
                        older = None
            if older is not None:
                run_to_end(older)
            older = newer
        if older is not None:
            run_to_end(older)
        flush_tail()
        kb.barrier(['sp', 'pool'])
        print("instr counts", kb.ninst, "dma sems", kb.nbuf, "sem counts", kb.cnt)
    return nc


_NC_CACHE = {}


def _layout_w_in(w):
    wk = w.reshape(2, 8, 128, DIN)
    mix = wk[..., :4096].reshape(2, 8, 128, 4, 8, 128)
    mix = mix.transpose(0, 4, 2, 1, 3, 5)
    xa = wk[..., 4096:].reshape(2, 8, 128, 4, 128).transpose(0, 2, 1, 3, 4)[:, None]
    out = np.concatenate([mix, xa], axis=1)
    return np.ascontiguousarray(out.reshape(2, 9, 128, 4096))


def kernel(**inputs):
    x = np.ascontiguousarray(np.asarray(inputs["x"], dtype=np.float32))
    mem = np.ascontiguousarray(np.asarray(inputs["mem"], dtype=np.float32))
    Bt, S, _ = x.shape
    nseq = Bt // NCORES
    key = (nseq, S)
    if key not in _NC_CACHE:
        _NC_CACHE[key] = build(nseq, S)
    nc = _NC_CACHE[key]
    shared = {
        "w_in": _layout_w_in(np.asarray(inputs["w_in"], dtype=np.float32)),
        "w_out": np.ascontiguousarray(np.asarray(inputs["w_out"], dtype=np.float32).reshape(2, 12, 128, D).transpose(0, 2, 1, 3).reshape(2, 128, 12 * D)),
        "ln_g": np.ascontiguousarray(np.asarray(inputs["ln_g"], dtype=np.float32)),
        "ln_b": np.ascontiguousarray(np.asarray(inputs["ln_b"], dtype=np.float32)),
        "hgrn_lb_logits": np.ascontiguousarray(np.asarray(inputs["hgrn_lb_logits"], dtype=np.float32)),
        "hgrn_norm_w": np.ascontiguousarray(np.asarray(inputs["hgrn_norm_w"], dtype=np.float32)),
        "conv_w": np.ascontiguousarray(np.asarray(inputs["conv_w"], dtype=np.float32)),
        "mem_ln_g": np.ascontiguousarray(np.asarray(inputs["mem_ln_g"], dtype=np.float32)).reshape(1, D),
        "mem_ln_b": np.ascontiguousarray(np.asarray(inputs["mem_ln_b"], dtype=np.float32)).reshape(1, D),
        "w_mem_kv": np.ascontiguousarray(np.asarray(inputs["w_mem_kv"], dtype=np.float32).reshape(8, 128, D).transpose(1, 0, 2).reshape(128, 8 * D)),
    }
    in_maps = []
    for c in range(NCORES):
        m = dict(shared)
        m["x"] = x[c * nseq:(c + 1) * nseq].reshape(nseq * S, D)
        m["mem"] = mem[c * nseq:(c + 1) * nseq].reshape(nseq * NMEM, D)
        in_maps.append(m)
    res = run_bass_kernel_spmd(nc, in_maps, core_ids=list(range(NCORES)))
    out = np.concatenate([r["y"].reshape(nseq, S, D) for r in res.results], axis=0)
    return out.astype(np.float32)
```
